# Optimizing a Trainium2 kernel written in Bass

```python
import math
import jax, jax.numpy as jnp
from jax import lax
import numpy as np

D_MODEL = 2048
BATCH = 2
SEQ = 16384
DEPTH = 1

CHUNK = 64
N_META = 16
EPS = 1e-6
D_SC = D_MODEL
SC_KERNEL = 3
MB_EXPAND = 2
D_INNER = MB_EXPAND * D_MODEL
MB_HEADDIM = 64
MB_HEADS = D_INNER // MB_HEADDIM
MB_GROUPS = 8
MB_STATE = 128
MB_CONV = 4
D_BC = MB_GROUPS * MB_STATE
D_XBC = D_INNER + 2 * D_BC
N_BRANCH = 2
D_FF = 5504
FFN_KERNEL = 3
IN_SIZES = (D_SC, D_SC, D_SC, D_INNER, D_XBC, MB_HEADS, N_BRANCH * D_MODEL)
D_IN_PROJ = 3 * D_SC + D_INNER + D_XBC + MB_HEADS + N_BRANCH * D_MODEL

kernel_name = "hybrid_shortconv_ssd_gated_block"


def rmsnorm(x, g):
    x32 = x.astype(jnp.float32)
    y = x32 * lax.rsqrt(jnp.mean(x32 * x32, axis=-1, keepdims=True) + EPS)
    return (y * g.astype(jnp.float32)).astype(x.dtype)


def causal_dwconv(x, w, b=None):
    K = w.shape[0]
    L = x.shape[1]
    xp = jnp.pad(x, ((0, 0), (K - 1, 0), (0, 0)))
    y = xp[:, K - 1:K - 1 + L] * w[K - 1]
    for k in range(K - 1):
        y = y + xp[:, k:k + L] * w[k]
    if b is not None:
        y = y + b
    return y


def split_cols(a, sizes):
    idx, run = [], 0
    for s in sizes[:-1]:
        run += s
        idx.append(run)
    return jnp.split(a, idx, axis=-1)


def ssd_chunked(xs, dt, A, Bm, Cm):
    b, T, H, P = xs.shape
    G, N = Bm.shape[2], Bm.shape[3]
    R = H // G
    nc = T // CHUNK
    f32 = jnp.float32
    x = xs.astype(f32).reshape(b, nc, CHUNK, G, R, P)
    dtc = dt.astype(f32).reshape(b, nc, CHUNK, G, R)
    Bc = Bm.astype(f32).reshape(b, nc, CHUNK, G, N)
    Cc = Cm.astype(f32).reshape(b, nc, CHUNK, G, N)
    acum = jnp.cumsum(dtc * A.astype(f32).reshape(G, R), axis=2)
    xdt = x * dtc[..., None]
    acum_t = jnp.moveaxis(acum, 2, -1)
    seg = acum_t[..., :, None] - acum_t[..., None, :]
    causal = jnp.tril(jnp.ones((CHUNK, CHUNK), dtype=bool))
    decay = jnp.exp(jnp.where(causal, seg, -jnp.inf))
    cb = jnp.einsum('bclgn,bcsgn->bcgls', Cc, Bc)
    y_diag = jnp.einsum('bcgrls,bcsgrp->bclgrp', decay * cb[:, :, :, None], xdt)
    a_last = acum[:, :, -1:]
    state_in = xdt * jnp.exp(a_last - acum)[..., None]
    chunk_decay = jnp.exp(a_last[:, :, 0])
    out_decay = jnp.exp(acum)

    def step(state, inp):
        Bk, Ck, sik, cdk, odk = inp
        y_off = jnp.einsum('blgn,bgrpn->blgrp', Ck, state) * odk[..., None]
        state = state * cdk[..., None, None] + jnp.einsum('blgn,blgrp->bgrpn', Bk, sik)
        return state, y_off

    to_c = lambda t: jnp.moveaxis(t, 1, 0)
    state0 = jnp.zeros((b, G, R, P, N), f32)
    _, y_off = lax.scan(step, state0, (to_c(Bc), to_c(Cc), to_c(state_in), to_c(chunk_decay), to_c(out_decay)))
    y = y_diag + jnp.moveaxis(y_off, 0, 1)
    return y.reshape(b, T, H, P).astype(xs.dtype)


def hybrid_layer(h, norm1_g, w_in, b_gate, sc_conv_w, mb_conv_w, mb_conv_b, dt_bias, a_log,
                 d_skip, mb_norm_g, w_a, w_m, w_o, norm2_g, w_up, ffn_conv_w, ffn_conv_b, w_down):
    bsz, L, _ = h.shape
    xn = rmsnorm(h, norm1_g)
    proj = xn @ w_in
    sc_b, sc_c, sc_h, z, xbc, dt_raw, gate_raw = split_cols(proj, IN_SIZES)

    y_a = sc_b * causal_dwconv(sc_c * sc_h, sc_conv_w)

    xbc = jax.nn.silu(causal_dwconv(xbc, mb_conv_w, mb_conv_b))
    xs, Bm, Cm = jnp.split(xbc, [D_INNER, D_INNER + D_BC], axis=-1)
    xs = xs.reshape(bsz, L, MB_HEADS, MB_HEADDIM)
    dt = jax.nn.softplus(dt_raw.astype(jnp.float32) + dt_bias.astype(jnp.float32))
    A = -jnp.exp(a_log.astype(jnp.float32))
    pad_left = (-L) % CHUNK
    padseq = lambda a: jnp.pad(a, [(0, 0), (pad_left, 0)] + [(0, 0)] * (a.ndim - 2))
    y_ssd = ssd_chunked(padseq(xs), padseq(dt), A,
                        padseq(Bm.reshape(bsz, L, MB_GROUPS, MB_STATE)),
                        padseq(Cm.reshape(bsz, L, MB_GROUPS, MB_STATE)))[:, pad_left:]
    y_m = (y_ssd + xs * d_skip[:, None].astype(xs.dtype)).reshape(bsz, L, D_INNER)
    y_m = y_m * jax.nn.silu(z)
    y_m = rmsnorm(y_m.reshape(bsz, L, MB_GROUPS, D_INNER // MB_GROUPS),
                  mb_norm_g.reshape(MB_GROUPS, D_INNER // MB_GROUPS)).reshape(bsz, L, D_INNER)

    g_a, g_m = jnp.split(jax.nn.sigmoid(gate_raw + b_gate), N_BRANCH, axis=-1)
    mix = g_a * (y_a @ w_a) + g_m * (y_m @ w_m)
    h = h + mix @ w_o

    xn2 = rmsnorm(h, norm2_g)
    u, v = jnp.split(xn2 @ w_up, 2, axis=-1)
    u = causal_dwconv(u, ffn_conv_w, ffn_conv_b)
    h = h + (jax.nn.silu(u) * v) @ w_down
    return h


def setup_inputs(seed: int = 0) -> dict:
    key = jax.random.key(seed)
    ks = jax.random.split(key, 24)
    f32 = jnp.float32
    nrm = lambda k, shape, s: jax.random.normal(k, shape, f32) * s
    x = nrm(ks[0], (BATCH, SEQ, D_MODEL), 1.0)
    meta_tokens = nrm(ks[1], (N_META, D_MODEL), 1.0)
    norm1_g = 1.0 + nrm(ks[2], (DEPTH, D_MODEL), 0.05)
    w_in = nrm(ks[3], (DEPTH, D_MODEL, D_IN_PROJ), D_MODEL ** -0.5)
    b_gate = nrm(ks[4], (DEPTH, N_BRANCH * D_MODEL), 0.1)
    sc_conv_w = nrm(ks[5], (DEPTH, SC_KERNEL, D_SC), SC_KERNEL ** -0.5)
    mb_conv_w = nrm(ks[6], (DEPTH, MB_CONV, D_XBC), MB_CONV ** -0.5)
    mb_conv_b = nrm(ks[7], (DEPTH, D_XBC), 0.01)
    dt0 = jnp.exp(jax.random.uniform(ks[8], (DEPTH, MB_HEADS), f32)
                  * (math.log(0.1) - math.log(0.001)) + math.log(0.001))
    dt_bias = dt0 + jnp.log(-jnp.expm1(-dt0))
    a_log = jnp.log(jax.random.uniform(ks[9], (DEPTH, MB_HEADS), f32, 1.0, 16.0))
    d_skip = 1.0 + nrm(ks[10], (DEPTH, MB_HEADS), 0.1)
    mb_norm_g = 1.0 + nrm(ks[11], (DEPTH, D_INNER), 0.05)
    w_a = nrm(ks[12], (DEPTH, D_SC, D_MODEL), D_SC ** -0.5)
    w_m = nrm(ks[13], (DEPTH, D_INNER, D_MODEL), D_INNER ** -0.5)
    w_o = nrm(ks[14], (DEPTH, D_MODEL, D_MODEL), D_MODEL ** -0.5)
    norm2_g = 1.0 + nrm(ks[15], (DEPTH, D_MODEL), 0.05)
    w_up = nrm(ks[16], (DEPTH, D_MODEL, 2 * D_FF), D_MODEL ** -0.5)
    ffn_conv_w = nrm(ks[17], (DEPTH, FFN_KERNEL, D_FF), FFN_KERNEL ** -0.5)
    ffn_conv_b = nrm(ks[18], (DEPTH, D_FF), 0.01)
    w_down = nrm(ks[19], (DEPTH, D_FF, D_MODEL), D_FF ** -0.5)
    normf_g = 1.0 + nrm(ks[20], (D_MODEL,), 0.05)
    return {"x": x, "meta_tokens": meta_tokens, "norm1_g": norm1_g, "w_in": w_in,
            "b_gate": b_gate, "sc_conv_w": sc_conv_w, "mb_conv_w": mb_conv_w,
            "mb_conv_b": mb_conv_b, "dt_bias": dt_bias, "a_log": a_log, "d_skip": d_skip,
            "mb_norm_g": mb_norm_g, "w_a": w_a, "w_m": w_m, "w_o": w_o, "norm2_g": norm2_g,
            "w_up": w_up, "ffn_conv_w": ffn_conv_w, "ffn_conv_b": ffn_conv_b,
            "w_down": w_down, "normf_g": normf_g}


def reference(x, meta_tokens, norm1_g, w_in, b_gate, sc_conv_w, mb_conv_w, mb_conv_b, dt_bias,
              a_log, d_skip, mb_norm_g, w_a, w_m, w_o, norm2_g, w_up, ffn_conv_w, ffn_conv_b,
              w_down, normf_g):
    bsz = x.shape[0]
    meta = jnp.broadcast_to(meta_tokens[None].astype(x.dtype), (bsz, N_META, D_MODEL))
    h = jnp.concatenate([meta, x], axis=1)
    for i in range(DEPTH):
        h = hybrid_layer(h, norm1_g[i], w_in[i], b_gate[i], sc_conv_w[i], mb_conv_w[i],
                         mb_conv_b[i], dt_bias[i], a_log[i], d_skip[i], mb_norm_g[i], w_a[i],
                         w_m[i], w_o[i], norm2_g[i], w_up[i], ffn_conv_w[i], ffn_conv_b[i],
                         w_down[i])
    return rmsnorm(h[:, N_META:], normf_g)
```

```python
import numpy as np
from contextlib import ExitStack
import concourse.bass as bass
import concourse.mybir as mybir
from concourse.bass_utils import run_bass_kernel_spmd

F32 = mybir.dt.float32
BF16 = mybir.dt.bfloat16
AF = mybir.ActivationFunctionType
ALU = mybir.AluOpType

D = 2048
KC = 16
DI = 4096
NG = 8
NH = 64
DFF = 5504
NFC = 43
EPS = 1e-6
N_META = 16
SEQ = 16384
BATCH = 2

S_SC, S_XBC, S_GATE, S_WA, S_WM, S_UP, NSU = 0, 48, 96, 128, 144, 176, 262
M_Z, M_WO, M_DN, NMU = 0, 32, 48, 92
C_G1, C_G2, C_SCW, C_MBW, C_MBB, C_FFW, C_FFB, C_BG, C_MBG, NCOLS = 0, 16, 32, 80, 272, 320, 449, 492, 524, 556

ENGS = ("pe", "act", "dve", "pool", "sp")
INORDER = ("pe", "act", "dve", "pool")


class Buf:
    __slots__ = ("name", "w", "r")

    def __init__(self, name):
        self.name = name
        self.w = None
        self.r = {}


class DSem:
    __slots__ = ("h", "count")

    def __init__(self, h):
        self.h = h
        self.count = 0


class Rec:
    __slots__ = ("eng", "fn", "deps", "need", "sem", "val", "dma", "idx")


class Prog:
    def __init__(self, nc, stack):
        self.nc = nc
        self.ops = {e: [] for e in ENGS}
        self.esem = {}
        for e in INORDER:
            self.esem[e] = stack.enter_context(nc.semaphore("s_" + e))
        self.stack = stack
        self.nrec = 0

    def dsem(self, name):
        return DSem(self.stack.enter_context(self.nc.semaphore(name)))

    def op(self, eng, fn, reads=(), writes=(), dsem=None):
        rec = Rec()
        rec.eng = eng
        rec.fn = fn
        rec.need = dsem is not None
        rec.dma = dsem
        rec.sem = None
        rec.val = 0
        rec.idx = self.nrec
        self.nrec += 1
        deps = {}
        wr = {}
        for b in reads:
            if b.w is not None:
                deps[id(b.w)] = b.w
                wr[id(b.w)] = True
        for b in writes:
            if b.w is not None:
                deps[id(b.w)] = b.w
                wr[id(b.w)] = True
            for r in b.r.values():
                deps[id(r)] = r
        out = []
        isdma = dsem is not None
        for k, d in deps.items():
            if d is rec:
                continue
            if d.dma is None and not isdma and d.eng == eng:
                if eng == "pe":
                    continue
                if k not in wr:
                    continue
            d.need = True
            out.append(d)
        rec.deps = out
        key = ("dma", rec.idx) if isdma else eng
        for b in reads:
            b.r[key] = rec
        for b in writes:
            b.w = rec
            b.r = {}
        self.ops[eng].append(rec)
        return rec

    def finalize(self):
        cnt = {e: 0 for e in INORDER}
        for e in ENGS:
            for rec in self.ops[e]:
                if rec.dma is not None:
                    rec.dma.count += 16
                    rec.sem = rec.dma.h
                    rec.val = rec.dma.count
                elif rec.need:
                    cnt[e] += 1
                    rec.sem = self.esem[e]
                    rec.val = cnt[e]
        self.cnt = cnt

    def emit_engine(self, e, eng, final_dsems=()):
        waited = {}
        for rec in self.ops[e]:
            for d in rec.deps:
                k = id(d.sem)
                if waited.get(k, 0) >= d.val:
                    continue
                waited[k] = d.val
                eng.wait_ge(d.sem, d.val)
            ins = rec.fn(eng)
            if rec.dma is not None:
                ins.then_inc(rec.sem, 16)
            elif rec.need:
                ins.then_inc(rec.sem, 1)
        for ds in final_dsems:
            if ds.count:
                eng.wait_ge(ds.h, ds.count)

    def run(self, final_dsems=()):
        self.finalize()
        nc = self.nc
        with nc.Block() as block:
            @block.tensor
            def _(e):
                self.emit_engine("pe", e)

            @block.scalar
            def _(e):
                self.emit_engine("act", e)

            @block.vector
            def _(e):
                self.emit_engine("dve", e)

            @block.gpsimd
            def _(e):
                self.emit_engine("pool", e, final_dsems)

            @block.sync
            def _(e):
                self.emit_engine("sp", e)


def switch_alias(old_bufs, new_bufs):
    accs = {}
    for b in old_bufs:
        if b.w is not None:
            accs[id(b.w)] = b.w
        for r in b.r.values():
            accs[id(r)] = r
    for n in new_bufs:
        n.w = None
        n.r = {("al", k): rec for k, rec in accs.items()}


def build(NT, NTB, CPB=4, NSLOT=6):
    NF = NT + 1
    NPFX = (CPB - 1) * NT
    NTILES = NPFX + NF
    assert NF % NTB == 0 and NPFX % NTB == 0
    T = NTB * 128
    nc = bass.Bass("TRN2", target_bir_lowering=False)
    xin = nc.dram_tensor("xin", [NTILES * 128, D], F32, kind="ExternalInput").ap()
    maskd = nc.dram_tensor("mask", [128, NTILES], F32, kind="ExternalInput").ap()
    WS = nc.dram_tensor("WS", [NSU, 128, 2048], F32, kind="ExternalInput").ap()
    WM = nc.dram_tensor("WM", [NMU, 128, 2048], F32, kind="ExternalInput").ap()
    WDT = nc.dram_tensor("WDT", [128, 1024], F32, kind="ExternalInput").ap()
    colsd = nc.dram_tensor("cols", [128, NCOLS], F32, kind="ExternalInput").ap()
    rowsd = nc.dram_tensor("rows", [192], F32, kind="ExternalInput").ap()
    gfd = nc.dram_tensor("gf", [D], F32, kind="ExternalInput").ap()
    outd = nc.dram_tensor("out", [NT * 128, D], F32, kind="ExternalOutput").ap()
    WSb = nc.dram_tensor("WSb", [NSU, 128, 2048], BF16, kind="Internal").ap()
    WMb = nc.dram_tensor("WMb", [NMU, 128, 2048], BF16, kind="Internal").ap()
    WDTb = nc.dram_tensor("WDTb", [128, 1024], BF16, kind="Internal").ap()

    with ExitStack() as st:
        P = Prog(nc, st)

        def sb(name, shape, dt):
            return st.enter_context(nc.sbuf_tensor(name, shape, dt))

        def MM(out, lhsT, rhs, start, stop, r, w):
            P.op("pe", lambda e: e.matmul(out, lhsT=lhsT, rhs=rhs, start=start, stop=stop), r, w)

        def TR(out, in_, ident, r, w):
            P.op("pe", lambda e: e.transpose(out=out, in_=in_, identity=ident), r, w)

        def ACT(out, in_, func, r, w, bias=None, scale=None, accum=None):
            kw = {}
            if bias is not None:
                kw["bias"] = bias
            if scale is not None:
                kw["scale"] = scale
            if accum is not None:
                kw["accum_out"] = accum
            P.op("act", lambda e: e.activation(out=out, in_=in_, func=func, **kw), r, w)

        def TT(eng, out, in0, in1, op, r, w):
            P.op(eng, lambda e: e.tensor_tensor(out=out, in0=in0, in1=in1, op=op), r, w)

        def TS(eng, out, in0, s1, s2, op0, op1, r, w):
            P.op(eng, lambda e: e.tensor_scalar(out=out, in0=in0, scalar1=s1, scalar2=s2, op0=op0, op1=op1), r, w)

        def STT(eng, out, in0, scalar, in1, op0, op1, r, w):
            P.op(eng, lambda e: e.scalar_tensor_tensor(out=out, in0=in0, scalar=scalar, in1=in1, op0=op0, op1=op1), r, w)

        def CP(eng, out, in_, r, w):
            if eng == "act":
                P.op("act", lambda e: e.activation(out=out, in_=in_, func=AF.Copy), r, w)
            else:
                P.op(eng, lambda e: e.tensor_copy(out=out, in_=in_), r, w)

        def MSET(eng, ap, val, w):
            P.op(eng, lambda e: e.memset(ap, val), (), w)

        def bc3(ap2, n, pos):
            m = ap2.shape[1]
            if pos == 2:
                return ap2.unsqueeze(2).to_broadcast([128, m, n])
            return ap2.unsqueeze(1).to_broadcast([128, n, m])

        def v3(ap2, b):
            return ap2.rearrange("p (a b) -> p a b", b=b)

        identf = sb("identf", [128, 128], F32); b_identf = Buf("identf")
        identb = sb("identb", [128, 128], BF16); b_identb = Buf("identb")
        triLE = sb("triLE", [128, 128], F32); b_tri = Buf("tri")
        U2 = sb("U2", [128, 128], F32); b_U2 = Buf("U2")
        onesf = sb("onesf", [128, 128], F32); b_ones = Buf("ones")
        cols = sb("cols_sb", [128, NCOLS], F32); b_cols = Buf("cols")
        rows = sb("rows_sb", [128, 192], F32); b_rows = Buf("rows")
        Abc = sb("Abc", [128, 64], F32); b_A = Buf("A")
        gfbc = sb("gfbc", [128, D], F32); b_gf = Buf("gf")
        maskt = sb("maskt", [128, NTILES], F32); b_mask = Buf("mask")
        ccol = sb("ccol", [128, 3], F32); b_ccol = Buf("ccol")
        junk = sb("junk", [128, D], BF16)
        dtb_bc = rows[:, 0:64]
        D_bc = rows[:, 128:192]

        d_c = [P.dsem("d_c%d" % k) for k in range(4)]
        P.op("sp", lambda e: e.dma_start(out=cols[:], in_=colsd[:, :]), (), [b_cols], dsem=d_c[0])
        P.op("sp", lambda e: e.dma_start(out=rows[:], in_=rowsd.partition_broadcast(128)), (), [b_rows], dsem=d_c[1])
        P.op("sp", lambda e: e.dma_start(out=gfbc[:], in_=gfd.partition_broadcast(128)), (), [b_gf], dsem=d_c[2])
        P.op("sp", lambda e: e.dma_start(out=maskt[:], in_=maskd[:, :]), (), [b_mask], dsem=d_c[3])

        MSET("pool", identf[:], 1.0, [b_identf])
        P.op("pool", lambda e: e.affine_select(out=identf[:], in_=identf[:], pattern=[[-1, 128]], compare_op=ALU.is_equal,
                                               fill=0.0, base=0, channel_multiplier=1), [b_identf], [b_identf])
        CP("dve", identb[:], identf[:], [b_identf], [b_identb])
        MSET("pool", triLE[:], 1.0, [b_tri])
        P.op("pool", lambda e: e.affine_select(out=triLE[:], in_=triLE[:], pattern=[[1, 128]], compare_op=ALU.is_ge,
                                               fill=0.0, base=0, channel_multiplier=-1), [b_tri], [b_tri])
        MSET("pool", U2[:], 1.0, [b_U2])
        P.op("pool", lambda e: e.affine_select(out=U2[:], in_=U2[:], pattern=[[-1, 128]], compare_op=ALU.is_gt,
                                               fill=0.0, base=0, channel_multiplier=1), [b_U2], [b_U2])
        MSET("dve", onesf[:], 1.0, [b_ones])
        MSET("dve", ccol[:, 0:1], 1.0, [b_ccol])
        MSET("dve", ccol[:, 1:2], EPS, [b_ccol])
        MSET("dve", ccol[:, 2:3], 0.0, [b_ccol])
        one_c = ccol[:, 0:1]
        eps_c = ccol[:, 1:2]
        zero_c = ccol[:, 2:3]
        ACT(Abc[:], rows[:, 64:128], AF.Exp, [b_rows], [b_A])
        TS("dve", Abc[:], Abc[:], -1.0, 0.0, ALU.mult, ALU.add, [b_A], [b_A])

        S = sb("S", [128, NG, 512], F32); b_S = [Buf("S%d" % g) for g in range(NG)]
        xbch = sb("xbch", [128, 48, 3], F32); b_xbch = [Buf("xbch%d" % j) for j in range(48)]
        sch = sb("sch", [128, 16, 2], F32); b_sch = [Buf("sch%d" % j) for j in range(16)]
        uh = sb("uh", [128, NFC, 2], F32); b_uh = [Buf("uh%d" % j) for j in range(NFC)]
        MSET("pool", S[:], 0.0, b_S)
        MSET("pool", xbch[:], 0.0, b_xbch)
        MSET("pool", sch[:], 0.0, b_sch)
        MSET("pool", uh[:], 0.0, b_uh)

        NSM = 64
        smt = sb("smt", [128, NSM], F32); b_sm = [Buf("sm%d" % k) for k in range(NSM)]
        sm_pos = [0]

        def sm():
            k = sm_pos[0] % NSM
            sm_pos[0] += 1
            return smt[:, k:k + 1], b_sm[k]

        dtt = sb("dtt", [128, NTB, 5, 64], F32)
        b_dt = [[Buf("dt%d_%d" % (i, k)) for k in range(5)] for i in range(NTB)]
        dtmp = sb("dtmp", [128, 4, 192], F32); b_dtmp = [Buf("dtmp%d" % k) for k in range(4)]

        xnT = sb("xnT", [128, KC, T], BF16); b_xnT = Buf("xnT")
        xld = [sb("xld%d" % k, [128, D], F32) for k in range(2)]; b_xld = [Buf("xld0"), Buf("xld1")]
        d_x = [P.dsem("d_x0"), P.dsem("d_x1")]
        d_o = [P.dsem("d_o0"), P.dsem("d_o1")]
        d_h = [P.dsem("d_h%d" % i) for i in range(NTB)]

        NRAW = 3
        rawt = [sb("raw%d" % k, [128, T + 3], F32) for k in range(NRAW)]
        b_rawh = [Buf("rawh%d" % k) for k in range(NRAW)]
        b_rawm = [Buf("rawm%d" % k) for k in range(NRAW)]
        acct = [sb("acc%d" % k, [128, T], F32) for k in range(NRAW)]
        b_acc = [Buf("acc%d" % k) for k in range(NRAW)]
        raw_pos = [0]
        ftmp = [sb("ftmp%d" % k, [128, T], F32) for k in range(2)]; b_ftmp = [Buf("ftmp0"), Buf("ftmp1")]
        ft_pos = [0]

        def ftbuf():
            k = ft_pos[0] % 2
            ft_pos[0] += 1
            return ftmp[k], b_ftmp[k]

        R1 = sb("R1", [128, 48 * T], BF16)
        ymT = v3(R1[:, 0:32 * T], T); b_ymT = Buf("ymT")
        yaT = v3(R1[:, 32 * T:48 * T], T); b_yaT = Buf("yaT")
        actT = v3(R1[:, 0:44 * T], T); b_actT = Buf("actT")

        off = [0]
        r2items = []

        def r2(nel_bf16):
            o = off[0]
            off[0] += nel_bf16
            return o

        o_xsB = [r2(NTB * 640) for _ in range(2)]
        o_BT = [r2(T) for _ in range(2)]
        o_CT = [r2(T) for _ in range(2)]
        o_sz = [r2(NTB * 512) for _ in range(2)]
        o_fT = [r2(T) for _ in range(2)]
        o_rs = [r2(2048) for _ in range(2)]
        o_E = [r2(1024) for _ in range(2)]
        o_MT = [r2(1024) for _ in range(2)]
        o_cbm = [r2(128) for _ in range(2)]
        o_xdt = [r2(512) for _ in range(2)]
        o_si = [r2(512) for _ in range(2)]
        o_Sb = [r2(512) for _ in range(2)]
        o_y = [r2(1024) for _ in range(4)]
        szA = off[0]
        szB = 16 * T + NTB * D * 2
        R2 = sb("R2", [128, max(szA, szB)], BF16)

        def r2v(o, n, dt=BF16):
            ap = R2[:, o:o + n]
            return ap.bitcast(F32) if dt == F32 else ap

        xsB_t = [v3(r2v(o, NTB * 640), 640) for o in o_xsB]; b_xsB = [Buf("xsB0"), Buf("xsB1")]
        BT_t = [r2v(o, T) for o in o_BT]; b_BT = [Buf("BT0"), Buf("BT1")]
        CT_t = [r2v(o, T) for o in o_CT]; b_CT = [Buf("CT0"), Buf("CT1")]
        sz_t = [v3(r2v(o, NTB * 512), 512) for o in o_sz]; b_sz = [Buf("sz0"), Buf("sz1")]
        fT_t = [r2v(o, T) for o in o_fT]; b_fT = [Buf("fT0"), Buf("fT1")]
        rs_t = [v3(r2v(o, 2048, F32), 128) for o in o_rs]; b_rs = [Buf("rs0"), Buf("rs1")]
        E_t = [v3(r2v(o, 1024), 128) for o in o_E]; b_E = [Buf("E0"), Buf("E1")]
        MT_t = [v3(r2v(o, 1024), 128) for o in o_MT]; b_MT = [Buf("MT0"), Buf("MT1")]
        cbm_t = [r2v(o, 128) for o in o_cbm]; b_cbm = [Buf("cbm0"), Buf("cbm1")]
        xdt_t = [r2v(o, 512) for o in o_xdt]; b_xdt = [Buf("xdt0"), Buf("xdt1")]
        si_t = [r2v(o, 512) for o in o_si]; b_si = [Buf("si0"), Buf("si1")]
        Sb_t = [r2v(o, 512) for o in o_Sb]; b_Sb = [Buf("Sb0"), Buf("Sb1")]
        y_t = [r2v(o, 1024, F32) for o in o_y]; b_y = [Buf("y%d" % k) for k in range(4)]
        R2A = b_xsB + b_BT + b_CT + b_sz + b_fT + b_rs + b_E + b_MT + b_cbm + b_xdt + b_si + b_Sb + b_y
        mixT = v3(R2[:, 0:16 * T], T); b_mixT = Buf("mixT")
        h1 = v3(R2[:, 16 * T:16 * T + NTB * D * 2].bitcast(F32), D); b_h1 = [Buf("h1_%d" % i) for i in range(NTB)]
        R2B = [b_mixT] + b_h1
        R1A = [b_ymT, b_yaT]
        R1B = [b_actT]
        rot = {}

        def rotate(name, n):
            k = rot.get(name, 0)
            rot[name] = k + 1
            return k % n

        pst = [st.enter_context(nc.psum_tensor("ps%d" % k, [128, 512], F32)) for k in range(8)]
        b_ps = [Buf("ps%d" % k) for k in range(8)]
        ps_pos = [0]

        def psum():
            k = ps_pos[0] % 8
            ps_pos[0] += 1
            return pst[k], b_ps[k]

        ring = [sb("ring%d" % s, [128, 2048], BF16) for s in range(NSLOT)]
        b_ring = [Buf("ring%d" % s) for s in range(NSLOT)]
        d_ring = [P.dsem("d_ring%d" % s) for s in range(NSLOT)]
        ring_pos = [0]
        NPC = 8
        d_pc = [P.dsem("d_pc%d" % k) for k in range(NPC)]
        b_pcs = [Buf("pcs%d" % k) for k in range(NPC)]
        b_wd = {}
        pc_pos = [0]

        pc_pending = []

        def pc_tick(n):
            for _ in range(n):
                if pc_pending:
                    precast_now(*pc_pending.pop(0))

        def precast(kind, idx, now=False):
            if now:
                precast_now(kind, idx)
            else:
                pc_pending.append((kind, idx))

        def precast_now(kind, idx):
            key = (kind, idx)
            if key in b_wd:
                return
            b = Buf("wd_%s%d" % (kind, idx))
            b_wd[key] = b
            k = pc_pos[0] % NPC
            pc_pos[0] += 1
            if kind == "S":
                src, dst = WS[idx], WSb[idx]
            elif kind == "M":
                src, dst = WM[idx], WMb[idx]
            else:
                src, dst = WDT[:, :], WDTb[:, :]
            P.op("pool", lambda e: e.dma_start(out=dst, in_=src), (), [b, b_pcs[k]], dsem=d_pc[k])

        def wload(kind, idx):
            s = ring_pos[0] % NSLOT
            ring_pos[0] += 1
            if kind == "S":
                src, n = WSb[idx], 2048
            elif kind == "M":
                src, n = WMb[idx], 2048
            else:
                src, n = WDTb[:, :], 1024
            dst = ring[s][:, 0:n]
            P.op("sp", lambda e: e.dma_start(out=dst, in_=src), [b_wd[(kind, idx)]], [b_ring[s]], dsem=d_ring[s])
            return ring[s], b_ring[s]

        precast("D", 0, True)
        for g in range(NG):
            for c in range(4):
                precast("S", S_XBC + g * 4 + c, True)
            precast("S", S_XBC + 32 + g, True)
        for g in range(NG):
            precast("S", S_XBC + 40 + g)
            for kg in range(4):
                precast("M", M_Z + g * 4 + kg)
        for i in range(48):
            precast("S", S_SC + i)
        for i in range(16):
            precast("S", S_WA + i)
            precast("S", S_WM + 2 * i)
            precast("S", S_WM + 2 * i + 1)
            precast("S", S_GATE + i)
            precast("S", S_GATE + 16 + i)
        for i in range(16):
            precast("M", M_WO + i)
        for i in range(86):
            precast("S", S_UP + i)
        for i in range(44):
            precast("M", M_DN + i)

        def rstd_from_ssq(ssq, bssq):
            lnv, bl = sm()
            ACT(lnv, ssq, AF.Ln, [bssq, b_ccol], [bl], bias=eps_c)
            rs_, br = sm()
            ACT(rs_, lnv, AF.Exp, [bl], [br], scale=-0.5)
            return rs_, br

        def norm_transpose(src, bsrc, dst_tm, bdst, gcol_off, dstT, bdstT, i):
            ssq, bssq = sm()
            ACT(junk[:], src, AF.Square, [bsrc], [bssq], scale=float(D ** -0.5), accum=ssq)
            rs_, br = rstd_from_ssq(ssq, bssq)
            ACT(dst_tm, src, AF.Copy, [bsrc, br], [bdst], scale=rs_)
            for q in range(4):
                ps, pb = psum()
                for kk in range(4):
                    c = 4 * q + kk
                    TR(ps[:, kk * 128:(kk + 1) * 128], dst_tm[:, c * 128:(c + 1) * 128], identf[:], [bdst, b_identf], [pb])
                TT("dve", dstT[:, 4 * q:4 * q + 4, i * 128:(i + 1) * 128], v3(ps[:, :], 128),
                   bc3(cols[:, gcol_off + 4 * q:gcol_off + 4 * q + 4], 128, 2), ALU.mult, [pb, b_cols], [bdstT])

        def proj_S(units, rhsT, brhs):
            ps, pb = psum()
            nk = len(units) * KC
            k = 0
            for ui, u in enumerate(units):
                slot, bs = wload("S", u)
                for kc in range(KC):
                    MM(ps[:, 0:T], slot[:, kc * 128:(kc + 1) * 128], rhsT[:, ui * KC + kc, :], k == 0, k == nk - 1,
                       [bs, brhs], [pb])
                    k += 1
            return ps, pb

        def conv(src_fn, K, wcol, bias, halo, bhalo):
            k = raw_pos[0] % NRAW
            raw_pos[0] += 1
            raw, bh, bm = rawt[k], b_rawh[k], b_rawm[k]
            acc, ba = acct[k], b_acc[k]
            CP("pool", raw[:, 0:K - 1], halo, [bhalo], [bh])
            src_fn(raw[:, K - 1:K - 1 + T], bm)
            CP("pool", halo, raw[:, T:T + K - 1], [bm], [bhalo])
            TS("dve", acc[:], raw[:, K - 1:K - 1 + T], wcol[:, K - 1:K], bias if bias is not None else zero_c,
               ALU.mult, ALU.add, [bm, b_cols, b_ccol], [ba])
            for kk in range(K - 1):
                STT("dve", acc[:], raw[:, kk:kk + T], wcol[:, kk:kk + 1], acc[:], ALU.mult, ALU.add,
                    [bm, bh, b_cols, ba], [ba])
            return acc, ba

        def dt_phase(t0):
            slot, bs = wload("D", 0)
            for i in range(NTB):
                tile = t0 + i
                ps, pb = psum()
                for kc in range(KC):
                    MM(ps[:, 0:64], xnT[:, kc, i * 128:(i + 1) * 128], slot[:, kc * 64:(kc + 1) * 64], kc == 0, kc == KC - 1,
                       [b_xnT, bs], [pb])
                k = rotate("dtmp", 4)
                tm, btm = dtmp[:, k, :], b_dtmp[k]
                xr, ax, ex, ln_ = tm[:, 0:64], tm[:, 64:128], tm[:, 128:192], tm[:, 64:128]
                dt_i, dtA_i, od_i, w_i, cd_i = [dtt[:, i, q, :] for q in range(5)]
                bdt, bdtA, bod, bw, bcd = b_dt[i]
                TT("dve", xr, ps[:, 0:64], dtb_bc, ALU.add, [pb, b_rows], [btm])
                ACT(ax, xr, AF.Abs, [btm], [btm])
                ACT(ex, ax, AF.Exp, [btm], [btm], scale=-1.0)
                ACT(ln_, ex, AF.Ln, [btm, b_ccol], [btm], bias=one_c)
                TS("dve", xr, xr, 0.0, 0.0, ALU.max, ALU.add, [btm], [btm])
                TT("dve", xr, xr, ln_, ALU.add, [btm], [btm])
                TS("dve", dt_i, xr, maskt[:, tile:tile + 1], zero_c, ALU.mult, ALU.add, [btm, b_mask, b_ccol], [bdt])
                TT("dve", dtA_i, dt_i, Abc[:], ALU.mult, [bdt, b_A], [bdtA])
                ps3, pb3 = psum()
                MM(ps3[:, 0:64], triLE[:], dtA_i, True, True, [b_tri, bdtA], [pb3])
                MM(ps3[:, 64:128], U2[:], dtA_i, True, True, [b_U2, bdtA], [pb3])
                MM(ps3[:, 128:192], onesf[:], dtA_i, True, True, [b_ones, bdtA], [pb3])
                ACT(od_i, ps3[:, 0:64], AF.Exp, [pb3], [bod])
                ACT(ex, ps3[:, 64:128], AF.Exp, [pb3], [btm])
                ACT(cd_i, ps3[:, 128:192], AF.Exp, [pb3], [bcd])
                TT("dve", w_i, ex, dt_i, ALU.mult, [btm, bdt], [bw])

        def ssd_tile(g, i, full, gb):
            hs0 = g * 8
            xsB, bx = xsB_t[gb], b_xsB[gb]
            xs_t = xsB[:, i, 0:512]
            xs3 = v3(xs_t, 64)
            B_t = xsB[:, i, 512:640]
            dt_i, dtA_i, od_i, w_i, cd_i = [dtt[:, i, q, hs0:hs0 + 8] for q in range(5)]
            bdt, bdtA, bod, bw, bcd = b_dt[i]
            tok = slice(i * 128, (i + 1) * 128)
            Sg = S[:, g, :]
            if full:
                k = rotate("rs", 2)
                rs_, brs = rs_t[k], b_rs[k]
                TT("pool", rs_, bc3(dtA_i, 128, 2), bc3(triLE[:, :], 8, 1), ALU.mult, [bdtA, b_tri], [brs])
                psA, pbA = psum()
                psB, pbB = psum()
                MM(psA[:, :], U2[:], rs_[:, 0:4, :], True, True, [b_U2, brs], [pbA])
                MM(psB[:, :], U2[:], rs_[:, 4:8, :], True, True, [b_U2, brs], [pbB])
                k = rotate("E", 2)
                E, bE = E_t[k], b_E[k]
                ACT(E[:, 0:4, :], v3(psA[:, :], 128), AF.Exp, [pbA], [bE])
                ACT(E[:, 4:8, :], v3(psB[:, :], 128), AF.Exp, [pbB], [bE])
                psc, pbc = psum()
                MM(psc[:, 0:128], BT_t[gb][:, tok], CT_t[gb][:, tok], True, True, [b_BT[gb], b_CT[gb]], [pbc])
                k = rotate("cbm", 2)
                cbm, bcbm = cbm_t[k], b_cbm[k]
                TT("dve", cbm, psc[:, 0:128], triLE[:], ALU.mult, [pbc, b_tri], [bcbm])
                k = rotate("MT", 2)
                MT, bMT = MT_t[k], b_MT[k]
                TT("dve", MT, E, bc3(cbm, 8, 1), ALU.mult, [bE, bcbm], [bMT])
                k = rotate("xdt", 2)
                xdt, bxdt = xdt_t[k], b_xdt[k]
                TT("pool", v3(xdt, 64), xs3, bc3(dt_i, 64, 2), ALU.mult, [bx, bdt], [bxdt])
                k = rotate("Sb", 2)
                Sb, bSb = Sb_t[k], b_Sb[k]
                CP("pool", Sb, Sg, [b_S[g]], [bSb])
                psy, pby = psum()
                for r in range(8):
                    MM(psy[:, r * 64:(r + 1) * 64], MT[:, r, :], xdt[:, r * 64:(r + 1) * 64], True, True, [bMT, bxdt], [pby])
                pso, pbo = psum()
                MM(pso[:, :], CT_t[gb][:, tok], Sb, True, True, [b_CT[gb], bSb], [pbo])
                k = rotate("y", 4)
                y1, by1 = y_t[k], b_y[k]
                k = rotate("y", 4)
                y2, by2 = y_t[k], b_y[k]
                TT("dve", v3(y1, 64), v3(pso[:, :], 64), bc3(od_i, 64, 2), ALU.mult, [pbo, bod], [by1])
                TT("dve", y1, y1, psy[:, :], ALU.add, [by1, pby], [by1])
                TT("pool", v3(y2, 64), xs3, bc3(D_bc[:, hs0:hs0 + 8], 64, 2), ALU.mult, [bx, b_rows], [by2])
                TT("pool", y1, y1, y2, ALU.add, [by1, by2], [by1])
                TT("pool", y1, y1, sz_t[gb][:, i, :], ALU.mult, [by1, b_sz[gb]], [by1])
                ssq, bssq = sm()
                ACT(junk[:, 0:512], y1, AF.Square, [by1], [bssq], scale=float(512 ** -0.5), accum=ssq)
                rstd, brstd = rstd_from_ssq(ssq, bssq)
                ACT(y2, y1, AF.Copy, [by1, brstd], [by2], scale=rstd)
                pt, pbt = psum()
                for kk in range(4):
                    TR(pt[:, kk * 128:(kk + 1) * 128], y2[:, kk * 128:(kk + 1) * 128], identf[:], [by2, b_identf], [pbt])
                TT("dve", ymT[:, 4 * g:4 * g + 4, tok], v3(pt[:, :], 128),
                   bc3(cols[:, C_MBG + 4 * g:C_MBG + 4 * g + 4], 128, 2), ALU.mult, [pbt, b_cols], [b_ymT])
            k = rotate("si", 2)
            si, bsi = si_t[k], b_si[k]
            TT("pool", v3(si, 64), xs3, bc3(w_i, 64, 2), ALU.mult, [bx, bw], [bsi])
            psu, pbu = psum()
            MM(psu[:, :], B_t, si, True, True, [bx, bsi], [pbu])
            TT("pool", v3(Sg, 64), v3(Sg, 64), bc3(cd_i, 64, 2), ALU.mult, [b_S[g], bcd], [b_S[g]])
            TT("dve", Sg, Sg, psu[:, :], ALU.add, [b_S[g], pbu], [b_S[g]])

        def group_phase(g, full):
            gb = g % 2
            xsB, bx = xsB_t[gb], b_xsB[gb]
            chunks = [("x", g * 4 + c, c) for c in range(4)] + [("B", 32 + g, 4)]
            if full:
                chunks.append(("C", 40 + g, 5))
            for kind, j, c in chunks:
                ps, pb = proj_S([S_XBC + j], xnT, b_xnT)

                def src_fn(dst, bm, ps=ps, pb=pb):
                    CP("act", dst, ps[:, 0:T], [pb], [bm])
                acc, ba = conv(src_fn, 4, cols[:, C_MBW + 4 * j:C_MBW + 4 * j + 4], cols[:, C_MBB + j:C_MBB + j + 1],
                               xbch[:, j, :], b_xbch[j])
                if kind == "C":
                    ACT(CT_t[gb], acc[:], AF.Silu, [ba], [b_CT[gb]])
                    continue
                if kind == "B":
                    fT, bfT = BT_t[gb], b_BT[gb]
                else:
                    k = rotate("fT", 2)
                    fT, bfT = fT_t[k], b_fT[k]
                ACT(fT, acc[:], AF.Silu, [ba], [bfT])
                pt, pbt = psum()
                ptb = v3(pt[:, :].bitcast(BF16)[:, 0:T], 128)
                for i in range(NTB):
                    TR(ptb[:, i, :], fT[:, i * 128:(i + 1) * 128], identb[:], [bfT, b_identb], [pbt])
                CP("dve", xsB[:, :, c * 128:(c + 1) * 128], ptb, [pbt], [bx])
            if full:
                zps = [psum() for _ in range(NTB)]
                for kg in range(4):
                    slot, bs = wload("M", M_Z + g * 4 + kg)
                    for i in range(NTB):
                        for kcl in range(4):
                            kc = kg * 4 + kcl
                            MM(zps[i][0][:, :], xnT[:, kc, i * 128:(i + 1) * 128], slot[:, kcl * 512:(kcl + 1) * 512],
                               kc == 0, kc == KC - 1, [b_xnT, bs], [zps[i][1]])
                for i in range(NTB):
                    ACT(sz_t[gb][:, i, :], zps[i][0][:, :], AF.Silu, [zps[i][1]], [b_sz[gb]])
            for i in range(NTB):
                ssd_tile(g, i, full, gb)

        def sc_phase():
            for i in range(16):
                psc, pbc = proj_S([S_SC + 3 * i], xnT, b_xnT)
                psh, pbh = proj_S([S_SC + 3 * i + 1], xnT, b_xnT)
                psb, pbb = proj_S([S_SC + 3 * i + 2], xnT, b_xnT)
                ft, bft = ftbuf()
                CP("act", ft[:], psc[:, 0:T], [pbc], [bft])

                def src_fn(dst, bm, ft=ft, bft=bft, psh=psh, pbh=pbh):
                    TT("dve", dst, ft[:], psh[:, 0:T], ALU.mult, [bft, pbh], [bm])
                acc, ba = conv(src_fn, 3, cols[:, C_SCW + 3 * i:C_SCW + 3 * i + 3], None, sch[:, i, :], b_sch[i])
                TT("dve", yaT[:, i, :], acc[:], psb[:, 0:T], ALU.mult, [ba, pbb], [b_yaT])

        def mix_phase():
            for i in range(16):
                psa, pba = proj_S([S_WA + i], yaT, b_yaT)
                psm, pbm = proj_S([S_WM + 2 * i, S_WM + 2 * i + 1], ymT, b_ymT)
                pga, pbga = proj_S([S_GATE + i], xnT, b_xnT)
                pgm, pbgm = proj_S([S_GATE + 16 + i], xnT, b_xnT)
                ga, bga = ftbuf()
                gm, bgm = ftbuf()
                ACT(ga[:], pga[:, 0:T], AF.Sigmoid, [pbga, b_cols], [bga], bias=cols[:, C_BG + i:C_BG + i + 1])
                ACT(gm[:], pgm[:, 0:T], AF.Sigmoid, [pbgm, b_cols], [bgm], bias=cols[:, C_BG + 16 + i:C_BG + 16 + i + 1])
                TT("dve", ga[:], ga[:], psa[:, 0:T], ALU.mult, [bga, pba], [bga])
                TT("dve", gm[:], gm[:], psm[:, 0:T], ALU.mult, [bgm, pbm], [bgm])
                TT("pool", mixT[:, i, :], ga[:], gm[:], ALU.add, [bga, bgm], [b_mixT])

        def tm_proj(unit0, nkg, nkc, lhsT_T, blhs, evac):
            for q in range(4):
                banks = [psum() for _ in range(NTB)]
                for kg in range(nkg):
                    slot, bs = wload("M", unit0 + q * nkg + kg)
                    for i in range(NTB):
                        for kcl in range(4):
                            kc = kg * 4 + kcl
                            if kc >= nkc:
                                continue
                            MM(banks[i][0][:, :], lhsT_T[:, kc, i * 128:(i + 1) * 128], slot[:, kcl * 512:(kcl + 1) * 512],
                               kc == 0, kc == nkc - 1, [blhs, bs], [banks[i][1]])
                for i in range(NTB):
                    evac(q, i, banks[i][0], banks[i][1])

        def ffn_up_phase():
            for i in range(NFC):
                psu, pbu = proj_S([S_UP + 2 * i], xnT, b_xnT)
                psv, pbv = proj_S([S_UP + 2 * i + 1], xnT, b_xnT)

                def src_fn(dst, bm, psu=psu, pbu=pbu):
                    CP("act", dst, psu[:, 0:T], [pbu], [bm])
                acc, ba = conv(src_fn, 3, cols[:, C_FFW + 3 * i:C_FFW + 3 * i + 3], cols[:, C_FFB + i:C_FFB + i + 1],
                               uh[:, i, :], b_uh[i])
                ft, bft = ftbuf()
                ACT(ft[:], acc[:], AF.Silu, [ba], [bft])
                TT("dve", actT[:, i, :], ft[:], psv[:, 0:T], ALU.mult, [bft, pbv], [b_actT])

        nblk_p = NPFX // NTB
        nblk_f = NF // NTB
        for blk in range(nblk_p + nblk_f):
            full = blk >= nblk_p
            t0 = blk * NTB
            if full:
                pc_tick(len(pc_pending))
            switch_alias(R2B, R2A)
            switch_alias(R1B, R1A)
            for i in range(NTB):
                tile = t0 + i
                k = tile % 2
                P.op("sp", lambda e, k=k, tile=tile: e.dma_start(out=xld[k][:], in_=xin[tile * 128:(tile + 1) * 128, :]),
                     (), [b_xld[k]], dsem=d_x[k])
                norm_transpose(xld[k][:], b_xld[k], xld[k][:], b_xld[k], C_G1, xnT, b_xnT, i)
            dt_phase(t0)
            for g in range(NG):
                group_phase(g, full)
            if not full:
                pc_tick(-(-320 // max(1, nblk_p - 1)))
                continue
            sc_phase()
            switch_alias(R2A, R2B)
            mix_phase()
            for i in range(NTB):
                tile = t0 + i
                P.op("pool", lambda e, i=i, tile=tile: e.dma_start(out=h1[:, i, :], in_=xin[tile * 128:(tile + 1) * 128, :]),
                     (), [b_h1[i]], dsem=d_h[i])

            def evac_h(q, i, ps, pb):
                TT("dve", h1[:, i, q * 512:(q + 1) * 512], ps[:, :], h1[:, i, q * 512:(q + 1) * 512], ALU.add,
                   [pb, b_h1[i]], [b_h1[i]])
            tm_proj(M_WO, 4, KC, mixT, b_mixT, evac_h)
            for i in range(NTB):
                k = i % 2
                norm_transpose(h1[:, i, :], b_h1[i], xld[k][:], b_xld[k], C_G2, xnT, b_xnT, i)
            switch_alias(R1A, R1B)
            ffn_up_phase()
            tm_proj(M_DN, 11, NFC, actT, b_actT, evac_h)
            for i in range(NTB):
                ftile = (blk - nblk_p) * NTB + i
                if ftile == 0:
                    continue
                k = i % 2
                ssq, bssq = sm()
                ACT(junk[:], h1[:, i, :], AF.Square, [b_h1[i]], [bssq], scale=float(D ** -0.5), accum=ssq)
                rstd, brstd = rstd_from_ssq(ssq, bssq)
                STT("dve", xld[k][:], h1[:, i, :], rstd, gfbc[:], ALU.mult, ALU.mult, [b_h1[i], brstd, b_gf], [b_xld[k]])
                orow = (ftile - 1) * 128
                P.op("pool", lambda e, k=k, orow=orow: e.dma_start(out=outd[orow:orow + 128, :], in_=xld[k][:]),
                     [b_xld[k]], (), dsem=d_o[k])
        P.run(final_dsems=d_o)
        nc._sem_counts = dict(P.cnt)
    return nc


def _s_unit(W, col0, k0=0):
    blk = W[k0:k0 + 2048, col0:col0 + 128]
    return blk.reshape(16, 128, 128).transpose(1, 0, 2).reshape(128, 2048)


def _m_unit(W, col0, k0):
    K = W.shape[0]
    blk = np.zeros((512, 512), np.float32)
    n = max(0, min(512, K - k0))
    blk[:n] = W[k0:k0 + n, col0:col0 + 512]
    return blk.reshape(4, 128, 512).transpose(1, 0, 2).reshape(128, 2048)


def _colp(v):
    return np.ascontiguousarray(v.reshape(-1, 128).T)


def _convp(w):
    K, C = w.shape
    return np.ascontiguousarray(w.T.reshape(C // 128, 128, K).transpose(1, 0, 2).reshape(128, -1))


def prep_weights(norm1_g, w_in, b_gate, sc_conv_w, mb_conv_w, mb_conv_b, dt_bias, a_log, d_skip, mb_norm_g,
                 w_a, w_m, w_o, norm2_g, w_up, ffn_conv_w, ffn_conv_b, w_down, normf_g):
    f = lambda a: np.asarray(a, np.float32)
    w_in, w_a, w_m, w_o, w_up, w_down = f(w_in)[0], f(w_a)[0], f(w_m)[0], f(w_o)[0], f(w_up)[0], f(w_down)[0]
    WS = np.empty((NSU, 128, 2048), np.float32)
    WM = np.empty((NMU, 128, 2048), np.float32)
    o_scb, o_scc, o_sch, o_z, o_xbc, o_dt, o_gate = 0, 2048, 4096, 6144, 10240, 16384, 16448
    for i in range(16):
        WS[S_SC + 3 * i] = _s_unit(w_in, o_scc + i * 128)
        WS[S_SC + 3 * i + 1] = _s_unit(w_in, o_sch + i * 128)
        WS[S_SC + 3 * i + 2] = _s_unit(w_in, o_scb + i * 128)
    for j in range(48):
        WS[S_XBC + j] = _s_unit(w_in, o_xbc + j * 128)
    for j in range(32):
        WS[S_GATE + j] = _s_unit(w_in, o_gate + j * 128)
    for i in range(16):
        WS[S_WA + i] = _s_unit(w_a, i * 128)
        WS[S_WM + 2 * i] = _s_unit(w_m, i * 128, 0)
        WS[S_WM + 2 * i + 1] = _s_unit(w_m, i * 128, 2048)
    for i in range(NFC):
        WS[S_UP + 2 * i] = _s_unit(w_up, i * 128)
        WS[S_UP + 2 * i + 1] = _s_unit(w_up, DFF + i * 128)
    for g in range(8):
        for kg in range(4):
            WM[M_Z + g * 4 + kg] = _m_unit(w_in, o_z + g * 512, kg * 512)
    for q in range(4):
        for kg in range(4):
            WM[M_WO + q * 4 + kg] = _m_unit(w_o, q * 512, kg * 512)
        for kg in range(11):
            WM[M_DN + q * 11 + kg] = _m_unit(w_down, q * 512, kg * 512)
    WDT = np.ascontiguousarray(w_in[:, o_dt:o_dt + 64].reshape(16, 128, 64).transpose(1, 0, 2).reshape(128, 1024))
    cols = np.zeros((128, NCOLS), np.float32)
    cols[:, C_G1:C_G1 + 16] = _colp(f(norm1_g)[0])
    cols[:, C_G2:C_G2 + 16] = _colp(f(norm2_g)[0])
    cols[:, C_SCW:C_SCW + 48] = _convp(f(sc_conv_w)[0])
    cols[:, C_MBW:C_MBW + 192] = _convp(f(mb_conv_w)[0])
    cols[:, C_MBB:C_MBB + 48] = _colp(f(mb_conv_b)[0])
    cols[:, C_FFW:C_FFW + 129] = _convp(f(ffn_conv_w)[0])
    cols[:, C_FFB:C_FFB + 43] = _colp(f(ffn_conv_b)[0])
    cols[:, C_BG:C_BG + 32] = _colp(f(b_gate)[0])
    cols[:, C_MBG:C_MBG + 32] = _colp(f(mb_norm_g)[0])
    rows = np.concatenate([f(dt_bias)[0], f(a_log)[0], f(d_skip)[0]]).astype(np.float32)
    return {"WS": WS, "WM": WM, "WDT": WDT, "cols": cols, "rows": rows, "gf": np.ascontiguousarray(f(normf_g))}


def prep_streams(x, meta_tokens, NT, CPB):
    x = np.asarray(x, np.float32)
    meta = np.asarray(meta_tokens, np.float32)
    B = x.shape[0]
    NPFX, NF = (CPB - 1) * NT, NT + 1
    NTILES = NPFX + NF
    streams, masks = [], []
    for b in range(B):
        seq = np.zeros(((CPB * NT + 1) * 128, D), np.float32)
        seq[128 - N_META:128] = meta
        seq[128:] = x[b]
        valid = np.ones(((CPB * NT + 1) * 128,), np.float32)
        valid[:128 - N_META] = 0.0
        for j in range(CPB):
            n_real = (j + 1) * NT + 1
            s = np.zeros((NTILES * 128, D), np.float32)
            m = np.zeros((NTILES * 128,), np.float32)
            s[(NTILES - n_real) * 128:] = seq[:n_real * 128]
            m[(NTILES - n_real) * 128:] = valid[:n_real * 128]
            streams.append(s)
            masks.append(np.ascontiguousarray(m.reshape(NTILES, 128).T))
    return streams, masks


_CACHE = {}


def run(inputs, NT, NTB, CPB):
    key = (NT, NTB, CPB)
    if key not in _CACHE:
        _CACHE[key] = build(NT, NTB, CPB)
    nc = _CACHE[key]
    wd = prep_weights(**{k: v for k, v in inputs.items() if k not in ("x", "meta_tokens")})
    streams, masks = prep_streams(inputs["x"], inputs["meta_tokens"], NT, CPB)
    B = 2
    ncores = B * CPB
    in_maps = []
    for c in range(ncores):
        m = dict(wd)
        m["xin"] = streams[c]
        m["mask"] = masks[c]
        in_maps.append(m)
    res = run_bass_kernel_spmd(nc, in_maps, core_ids=list(range(ncores)))
    out = np.empty((B, CPB * NT * 128, D), np.float32)
    for c in range(ncores):
        b, j = divmod(c, CPB)
        out[b, j * NT * 128:(j + 1) * NT * 128] = res.results[c]["out"]
    return out


CPB_FULL = 1


def kernel(**inputs):
    return run(inputs, SEQ // 128 // CPB_FULL, 3, CPB_FULL)
```

```python
import numpy as np
from contextlib import ExitStack
import concourse.bass as bass
import concourse.mybir as mybir
from concourse.bass_utils import run_bass_kernel_spmd

F32 = mybir.dt.float32
BF16 = mybir.dt.bfloat16
AF = mybir.ActivationFunctionType
ALU = mybir.AluOpType

D = 2048
KC = 16
DI = 4096
NG = 8
NH = 64
DFF = 5504
NFC = 43
EPS = 1e-6
N_META = 16
SEQ = 16384
BATCH = 2

S_SC, S_XBC, S_GATE, S_WA, S_WM, S_UP, NSU = 0, 48, 96, 128, 144, 176, 262
M_Z, M_WO, M_DN, NMU = 0, 32, 48, 92
C_G1, C_G2, C_SCW, C_MBW, C_MBB, C_FFW, C_FFB, C_BG, C_MBG, NCOLS = 0, 16, 32, 80, 272, 320, 449, 492, 524, 556

ENGS = ("pe", "act", "dve", "pool", "sp")
INORDER = ("pe", "act", "dve", "pool")


class Buf:
    __slots__ = ("name", "w", "r")

    def __init__(self, name):
        self.name = name
        self.w = None
        self.r = {}


class DSem:
    __slots__ = ("h", "count")

    def __init__(self, h):
        self.h = h
        self.count = 0


class Rec:
    __slots__ = ("eng", "fn", "deps", "need", "sem", "val", "dma", "idx")


class Prog:
    def __init__(self, nc, stack):
        self.nc = nc
        self.ops = {e: [] for e in ENGS}
        self.esem = {}
        for e in INORDER:
            self.esem[e] = stack.enter_context(nc.semaphore("s_" + e))
        self.stack = stack
        self.nrec = 0

    def dsem(self, name):
        return DSem(self.stack.enter_context(self.nc.semaphore(name)))

    def op(self, eng, fn, reads=(), writes=(), dsem=None):
        rec = Rec()
        rec.eng = eng
        rec.fn = fn
        rec.need = dsem is not None
        rec.dma = dsem
        rec.sem = None
        rec.val = 0
        rec.idx = self.nrec
        self.nrec += 1
        deps = {}
        wr = {}
        for b in reads:
            if b.w is not None:
                deps[id(b.w)] = b.w
                wr[id(b.w)] = True
        for b in writes:
            if b.w is not None:
                deps[id(b.w)] = b.w
                wr[id(b.w)] = True
            for r in b.r.values():
                deps[id(r)] = r
        out = []
        isdma = dsem is not None
        for k, d in deps.items():
            if d is rec:
                continue
            if d.dma is None and not isdma and d.eng == eng:
                if eng == "pe":
                    continue
                if k not in wr:
                    continue
            d.need = True
            out.append(d)
        rec.deps = out
        key = ("dma", rec.idx) if isdma else eng
        for b in reads:
            b.r[key] = rec
        for b in writes:
            b.w = rec
            b.r = {}
        self.ops[eng].append(rec)
        return rec

    def finalize(self):
        cnt = {e: 0 for e in INORDER}
        for e in ENGS:
            for rec in self.ops[e]:
                if rec.dma is not None:
                    rec.dma.count += 16
                    rec.sem = rec.dma.h
                    rec.val = rec.dma.count
                elif rec.need:
                    cnt[e] += 1
                    rec.sem = self.esem[e]
                    rec.val = cnt[e]
        self.cnt = cnt

    def emit_engine(self, e, eng, final_dsems=()):
        waited = {}
        for rec in self.ops[e]:
            for d in rec.deps:
                k = id(d.sem)
                if waited.get(k, 0) >= d.val:
                    continue
                waited[k] = d.val
                eng.wait_ge(d.sem, d.val)
            ins = rec.fn(eng)
            if rec.dma is not None:
                ins.then_inc(rec.sem, 16)
            elif rec.need:
                ins.then_inc(rec.sem, 1)
        for ds in final_dsems:
            if ds.count:
                eng.wait_ge(ds.h, ds.count)

    def run(self, final_dsems=()):
        self.finalize()
        nc = self.nc
        with nc.Block() as block:
            @block.tensor
            def _(e):
                self.emit_engine("pe", e)

            @block.scalar
            def _(e):
                self.emit_engine("act", e)

            @block.vector
            def _(e):
                self.emit_engine("dve", e)

            @block.gpsimd
            def _(e):
                self.emit_engine("pool", e, final_dsems)

            @block.sync
            def _(e):
                self.emit_engine("sp", e)


def switch_alias(old_bufs, new_bufs):
    accs = {}
    for b in old_bufs:
        if b.w is not None:
            accs[id(b.w)] = b.w
        for r in b.r.values():
            accs[id(r)] = r
    for n in new_bufs:
        n.w = None
        n.r = {("al", k): rec for k, rec in accs.items()}


def _rup(a, b):
    return -(-a // b) * b


def build(NT, NTB, CPB=4, NSLOT=8):
    NF = _rup(NT + 1, NTB)
    NPFX = _rup((CPB - 1) * NT, NTB)
    NTILES = NPFX + NF
    assert NF % NTB == 0 and NPFX % NTB == 0
    T = NTB * 128
    nc = bass.Bass("TRN2", target_bir_lowering=False)
    xin = nc.dram_tensor("xin", [NTILES * 128, D], F32, kind="ExternalInput").ap()
    maskd = nc.dram_tensor("mask", [128, NTILES], F32, kind="ExternalInput").ap()
    WS = nc.dram_tensor("WS", [NSU, 128, 2048], F32, kind="ExternalInput").ap()
    WM = nc.dram_tensor("WM", [NMU, 128, 2048], F32, kind="ExternalInput").ap()
    WDT = nc.dram_tensor("WDT", [128, 1024], F32, kind="ExternalInput").ap()
    colsd = nc.dram_tensor("cols", [128, NCOLS], F32, kind="ExternalInput").ap()
    rowsd = nc.dram_tensor("rows", [192], F32, kind="ExternalInput").ap()
    gfd = nc.dram_tensor("gf", [D], F32, kind="ExternalInput").ap()
    outd = nc.dram_tensor("out", [NT * 128, D], F32, kind="ExternalOutput").ap()
    WSb = nc.dram_tensor("WSb", [NSU, 128, 2048], BF16, kind="Internal").ap()
    WMb = nc.dram_tensor("WMb", [NMU, 128, 2048], BF16, kind="Internal").ap()
    WDTb = nc.dram_tensor("WDTb", [128, 1024], BF16, kind="Internal").ap()

    with ExitStack() as st:
        P = Prog(nc, st)

        def sb(name, shape, dt):
            return st.enter_context(nc.sbuf_tensor(name, shape, dt))

        def MM(out, lhsT, rhs, start, stop, r, w):
            P.op("pe", lambda e: e.matmul(out, lhsT=lhsT, rhs=rhs, start=start, stop=stop), r, w)

        def TR(out, in_, ident, r, w):
            P.op("pe", lambda e: e.transpose(out=out, in_=in_, identity=ident), r, w)

        def ACT(out, in_, func, r, w, bias=None, scale=None, accum=None):
            kw = {}
            if bias is not None:
                kw["bias"] = bias
            if scale is not None:
                kw["scale"] = scale
            if accum is not None:
                kw["accum_out"] = accum
            P.op("act", lambda e: e.activation(out=out, in_=in_, func=func, **kw), r, w)

        def TT(eng, out, in0, in1, op, r, w):
            P.op(eng, lambda e: e.tensor_tensor(out=out, in0=in0, in1=in1, op=op), r, w)

        def TS(eng, out, in0, s1, s2, op0, op1, r, w):
            P.op(eng, lambda e: e.tensor_scalar(out=out, in0=in0, scalar1=s1, scalar2=s2, op0=op0, op1=op1), r, w)

        def STT(eng, out, in0, scalar, in1, op0, op1, r, w):
            P.op(eng, lambda e: e.scalar_tensor_tensor(out=out, in0=in0, scalar=scalar, in1=in1, op0=op0, op1=op1), r, w)

        def CP(eng, out, in_, r, w):
            if eng == "act":
                P.op("act", lambda e: e.activation(out=out, in_=in_, func=AF.Copy), r, w)
            else:
                P.op(eng, lambda e: e.tensor_copy(out=out, in_=in_), r, w)

        def MSET(eng, ap, val, w):
            P.op(eng, lambda e: e.memset(ap, val), (), w)

        def bc3(ap2, n, pos):
            m = ap2.shape[1]
            if pos == 2:
                return ap2.unsqueeze(2).to_broadcast([128, m, n])
            return ap2.unsqueeze(1).to_broadcast([128, n, m])

        def v3(ap2, b):
            return ap2.rearrange("p (a b) -> p a b", b=b)

        identf = sb("identf", [128, 128], F32); b_identf = Buf("identf")
        identb = sb("identb", [128, 128], BF16); b_identb = Buf("identb")
        triLE = sb("triLE", [128, 128], F32); b_tri = Buf("tri")
        U2 = sb("U2", [128, 128], F32); b_U2 = Buf("U2")
        onesf = sb("onesf", [128, 128], F32); b_ones = Buf("ones")
        cols = sb("cols_sb", [128, NCOLS], F32); b_cols = Buf("cols")
        rows = sb("rows_sb", [128, 192], F32); b_rows = Buf("rows")
        Abc = sb("Abc", [128, 64], F32); b_A = Buf("A")
        gfbc = sb("gfbc", [128, D], F32); b_gf = Buf("gf")
        maskt = sb("maskt", [128, NTILES], F32); b_mask = Buf("mask")
        ccol = sb("ccol", [128, 3], F32); b_ccol = Buf("ccol")
        junk = sb("junk", [128, D], BF16)
        dtb_bc = rows[:, 0:64]
        D_bc = rows[:, 128:192]

        d_c = [P.dsem("d_c%d" % k) for k in range(4)]
        P.op("sp", lambda e: e.dma_start(out=cols[:], in_=colsd[:, :]), (), [b_cols], dsem=d_c[0])
        P.op("sp", lambda e: e.dma_start(out=rows[:], in_=rowsd.partition_broadcast(128)), (), [b_rows], dsem=d_c[1])
        P.op("sp", lambda e: e.dma_start(out=gfbc[:], in_=gfd.partition_broadcast(128)), (), [b_gf], dsem=d_c[2])
        P.op("sp", lambda e: e.dma_start(out=maskt[:], in_=maskd[:, :]), (), [b_mask], dsem=d_c[3])

        MSET("pool", identf[:], 1.0, [b_identf])
        P.op("pool", lambda e: e.affine_select(out=identf[:], in_=identf[:], pattern=[[-1, 128]], compare_op=ALU.is_equal,
                                               fill=0.0, base=0, channel_multiplier=1), [b_identf], [b_identf])
        CP("dve", identb[:], identf[:], [b_identf], [b_identb])
        MSET("pool", triLE[:], 1.0, [b_tri])
        P.op("pool", lambda e: e.affine_select(out=triLE[:], in_=triLE[:], pattern=[[1, 128]], compare_op=ALU.is_ge,
                                               fill=0.0, base=0, channel_multiplier=-1), [b_tri], [b_tri])
        MSET("pool", U2[:], 1.0, [b_U2])
        P.op("pool", lambda e: e.affine_select(out=U2[:], in_=U2[:], pattern=[[-1, 128]], compare_op=ALU.is_gt,
                                               fill=0.0, base=0, channel_multiplier=1), [b_U2], [b_U2])
        MSET("dve", onesf[:], 1.0, [b_ones])
        MSET("dve", ccol[:, 0:1], 1.0, [b_ccol])
        MSET("dve", ccol[:, 1:2], EPS, [b_ccol])
        MSET("dve", ccol[:, 2:3], 0.0, [b_ccol])
        one_c = ccol[:, 0:1]
        eps_c = ccol[:, 1:2]
        zero_c = ccol[:, 2:3]
        ACT(Abc[:], rows[:, 64:128], AF.Exp, [b_rows], [b_A])
        TS("dve", Abc[:], Abc[:], -1.0, 0.0, ALU.mult, ALU.add, [b_A], [b_A])

        S = sb("S", [128, NG, 512], F32); b_S = [Buf("S%d" % g) for g in range(NG)]
        xbch = sb("xbch", [128, 48, 3], F32); b_xbch = [Buf("xbch%d" % j) for j in range(48)]
        sch = sb("sch", [128, 16, 2], F32); b_sch = [Buf("sch%d" % j) for j in range(16)]
        uh = sb("uh", [128, NFC, 2], F32); b_uh = [Buf("uh%d" % j) for j in range(NFC)]
        MSET("pool", S[:], 0.0, b_S)
        MSET("pool", xbch[:], 0.0, b_xbch)
        MSET("pool", sch[:], 0.0, b_sch)
        MSET("pool", uh[:], 0.0, b_uh)

        NSM = 64
        smt = sb("smt", [128, NSM], F32); b_sm = [Buf("sm%d" % k) for k in range(NSM)]
        sm_pos = [0]

        def sm():
            k = sm_pos[0] % NSM
            sm_pos[0] += 1
            return smt[:, k:k + 1], b_sm[k]

        dtt = sb("dtt", [128, NTB, 5, 64], F32)
        b_dt = [[Buf("dt%d_%d" % (i, k)) for k in range(5)] for i in range(NTB)]
        dtmp = sb("dtmp", [128, 4, 192], F32); b_dtmp = [Buf("dtmp%d" % k) for k in range(4)]

        xnT = sb("xnT", [128, KC, T], BF16); b_xnT = Buf("xnT")
        xld = [sb("xld%d" % k, [128, D], F32) for k in range(2)]; b_xld = [Buf("xld0"), Buf("xld1")]
        d_x = [P.dsem("d_x0"), P.dsem("d_x1")]
        d_o = [P.dsem("d_o0"), P.dsem("d_o1")]
        d_h = [P.dsem("d_h%d" % i) for i in range(NTB)]

        NRAW = 3
        rawt = [sb("raw%d" % k, [128, T + 3], F32) for k in range(NRAW)]
        b_rawh = [Buf("rawh%d" % k) for k in range(NRAW)]
        b_rawm = [Buf("rawm%d" % k) for k in range(NRAW)]
        acct = [sb("acc%d" % k, [128, T], F32) for k in range(NRAW)]
        b_acc = [Buf("acc%d" % k) for k in range(NRAW)]
        raw_pos = [0]
        ftmp = [sb("ftmp%d" % k, [128, T], F32) for k in range(2)]; b_ftmp = [Buf("ftmp0"), Buf("ftmp1")]
        ft_pos = [0]

        def ftbuf():
            k = ft_pos[0] % 2
            ft_pos[0] += 1
            return ftmp[k], b_ftmp[k]

        R1 = sb("R1", [128, 48 * T], BF16)
        ymT = v3(R1[:, 0:32 * T], T); b_ymT = Buf("ymT")
        yaT = v3(R1[:, 32 * T:48 * T], T); b_yaT = Buf("yaT")
        actT = v3(R1[:, 0:44 * T], T); b_actT = Buf("actT")

        off = [0]
        r2items = []

        def r2(nel_bf16):
            o = off[0]
            off[0] += nel_bf16
            return o

        o_xsB = [r2(NTB * 640) for _ in range(2)]
        o_BT = [r2(T) for _ in range(2)]
        o_CT = [r2(T) for _ in range(2)]
        o_sz = [r2(NTB * 512) for _ in range(2)]
        o_fT = [r2(T) for _ in range(2)]
        o_rs = [r2(2048) for _ in range(2)]
        o_E = [r2(1024) for _ in range(2)]
        o_MT = [r2(1024) for _ in range(2)]
        o_cbm = [r2(128) for _ in range(2)]
        o_xdt = [r2(512) for _ in range(2)]
        o_si = [r2(512) for _ in range(2)]
        o_Sb = [r2(512) for _ in range(2)]
        o_y = [r2(1024) for _ in range(4)]
        szA = off[0]
        szB = 16 * T + NTB * D * 2
        R2 = sb("R2", [128, max(szA, szB)], BF16)

        def r2v(o, n, dt=BF16):
            ap = R2[:, o:o + n]
            return ap.bitcast(F32) if dt == F32 else ap

        xsB_t = [v3(r2v(o, NTB * 640), 640) for o in o_xsB]; b_xsB = [Buf("xsB0"), Buf("xsB1")]
        BT_t = [r2v(o, T) for o in o_BT]; b_BT = [Buf("BT0"), Buf("BT1")]
        CT_t = [r2v(o, T) for o in o_CT]; b_CT = [Buf("CT0"), Buf("CT1")]
        sz_t = [v3(r2v(o, NTB * 512), 512) for o in o_sz]; b_sz = [Buf("sz0"), Buf("sz1")]
        fT_t = [r2v(o, T) for o in o_fT]; b_fT = [Buf("fT0"), Buf("fT1")]
        rs_t = [v3(r2v(o, 2048, F32), 128) for o in o_rs]; b_rs = [Buf("rs0"), Buf("rs1")]
        E_t = [v3(r2v(o, 1024), 128) for o in o_E]; b_E = [Buf("E0"), Buf("E1")]
        MT_t = [v3(r2v(o, 1024), 128) for o in o_MT]; b_MT = [Buf("MT0"), Buf("MT1")]
        cbm_t = [r2v(o, 128) for o in o_cbm]; b_cbm = [Buf("cbm0"), Buf("cbm1")]
        xdt_t = [r2v(o, 512) for o in o_xdt]; b_xdt = [Buf("xdt0"), Buf("xdt1")]
        si_t = [r2v(o, 512) for o in o_si]; b_si = [Buf("si0"), Buf("si1")]
        Sb_t = [r2v(o, 512) for o in o_Sb]; b_Sb = [Buf("Sb0"), Buf("Sb1")]
        y_t = [r2v(o, 1024, F32) for o in o_y]; b_y = [Buf("y%d" % k) for k in range(4)]
        R2A = b_xsB + b_BT + b_CT + b_sz + b_fT + b_rs + b_E + b_MT + b_cbm + b_xdt + b_si + b_Sb + b_y
        mixT = v3(R2[:, 0:16 * T], T); b_mixT = Buf("mixT")
        h1 = v3(R2[:, 16 * T:16 * T + NTB * D * 2].bitcast(F32), D); b_h1 = [Buf("h1_%d" % i) for i in range(NTB)]
        R2B = [b_mixT] + b_h1
        R1A = [b_ymT, b_yaT]
        R1B = [b_actT]
        rot = {}

        def rotate(name, n):
            k = rot.get(name, 0)
            rot[name] = k + 1
            return k % n

        pst = [st.enter_context(nc.psum_tensor("ps%d" % k, [128, 512], F32)) for k in range(8)]
        b_ps = [Buf("ps%d" % k) for k in range(8)]
        ps_pos = [0]

        def psum():
            k = ps_pos[0] % 8
            ps_pos[0] += 1
            return pst[k], b_ps[k]

        ring = [sb("ring%d" % s, [128, 2048], BF16) for s in range(NSLOT)]
        b_ring = [Buf("ring%d" % s) for s in range(NSLOT)]
        d_ring = [P.dsem("d_ring%d" % s) for s in range(NSLOT)]
        ring_pos = [0]
        NPC = 8
        d_pc = [P.dsem("d_pc%d" % k) for k in range(NPC)]
        b_pcs = [Buf("pcs%d" % k) for k in range(NPC)]
        b_wd = {}
        pc_pos = [0]

        pc_pending = []

        def pc_tick(n):
            for _ in range(n):
                if pc_pending:
                    precast_now(*pc_pending.pop(0))

        def precast(kind, idx, now=False):
            if now:
                precast_now(kind, idx)
            else:
                pc_pending.append((kind, idx))

        def precast_now(kind, idx):
            key = (kind, idx)
            if key in b_wd:
                return
            b = Buf("wd_%s%d" % (kind, idx))
            b_wd[key] = b
            k = pc_pos[0] % NPC
            pc_pos[0] += 1
            if kind == "S":
                src, dst = WS[idx], WSb[idx]
            elif kind == "M":
                src, dst = WM[idx], WMb[idx]
            else:
                src, dst = WDT[:, :], WDTb[:, :]
            P.op("pool", lambda e: e.dma_start(out=dst, in_=src), (), [b, b_pcs[k]], dsem=d_pc[k])

        def wload(kind, idx):
            s = ring_pos[0] % NSLOT
            ring_pos[0] += 1
            if kind == "S":
                src, n = WSb[idx], 2048
            elif kind == "M":
                src, n = WMb[idx], 2048
            else:
                src, n = WDTb[:, :], 1024
            dst = ring[s][:, 0:n]
            P.op("sp", lambda e: e.dma_start(out=dst, in_=src), [b_wd[(kind, idx)]], [b_ring[s]], dsem=d_ring[s])
            return ring[s], b_ring[s]

        precast("D", 0, True)
        for g in range(NG):
            for c in range(4):
                precast("S", S_XBC + g * 4 + c, True)
            precast("S", S_XBC + 32 + g, True)
        for g in range(NG):
            precast("S", S_XBC + 40 + g)
            for kg in range(4):
                precast("M", M_Z + g * 4 + kg)
        for i in range(48):
            precast("S", S_SC + i)
        for i in range(16):
            precast("S", S_WA + i)
            precast("S", S_WM + 2 * i)
            precast("S", S_WM + 2 * i + 1)
            precast("S", S_GATE + i)
            precast("S", S_GATE + 16 + i)
        for i in range(16):
            precast("M", M_WO + i)
        for i in range(86):
            precast("S", S_UP + i)
        for i in range(44):
            precast("M", M_DN + i)

        def rstd_from_ssq(ssq, bssq):
            lnv, bl = sm()
            ACT(lnv, ssq, AF.Ln, [bssq, b_ccol], [bl], bias=eps_c)
            rs_, br = sm()
            ACT(rs_, lnv, AF.Exp, [bl], [br], scale=-0.5)
            return rs_, br

        def norm_transpose(src, bsrc, dst_tm, bdst, gcol_off, dstT, bdstT, i):
            ssq, bssq = sm()
            ACT(junk[:], src, AF.Square, [bsrc], [bssq], scale=float(D ** -0.5), accum=ssq)
            rs_, br = rstd_from_ssq(ssq, bssq)
            ACT(dst_tm, src, AF.Copy, [bsrc, br], [bdst], scale=rs_)
            for q in range(4):
                ps, pb = psum()
                for kk in range(4):
                    c = 4 * q + kk
                    TR(ps[:, kk * 128:(kk + 1) * 128], dst_tm[:, c * 128:(c + 1) * 128], identf[:], [bdst, b_identf], [pb])
                TT("dve", dstT[:, 4 * q:4 * q + 4, i * 128:(i + 1) * 128], v3(ps[:, :], 128),
                   bc3(cols[:, gcol_off + 4 * q:gcol_off + 4 * q + 4], 128, 2), ALU.mult, [pb, b_cols], [bdstT])

        def proj_S(units, rhsT, brhs):
            ps, pb = psum()
            nk = len(units) * KC
            k = 0
            for ui, u in enumerate(units):
                slot, bs = wload("S", u)
                for kc in range(KC):
                    MM(ps[:, 0:T], slot[:, kc * 128:(kc + 1) * 128], rhsT[:, ui * KC + kc, :], k == 0, k == nk - 1,
                       [bs, brhs], [pb])
                    k += 1
            return ps, pb

        def conv(src_fn, K, wcol, bias, halo, bhalo):
            k = raw_pos[0] % NRAW
            raw_pos[0] += 1
            raw, bh, bm = rawt[k], b_rawh[k], b_rawm[k]
            acc, ba = acct[k], b_acc[k]
            CP("pool", raw[:, 0:K - 1], halo, [bhalo], [bh])
            src_fn(raw[:, K - 1:K - 1 + T], bm)
            CP("pool", halo, raw[:, T:T + K - 1], [bm], [bhalo])
            TS("dve", acc[:], raw[:, K - 1:K - 1 + T], wcol[:, K - 1:K], bias if bias is not None else zero_c,
               ALU.mult, ALU.add, [bm, b_cols, b_ccol], [ba])
            for kk in range(K - 1):
                STT("dve", acc[:], raw[:, kk:kk + T], wcol[:, kk:kk + 1], acc[:], ALU.mult, ALU.add,
                    [bm, bh, b_cols, ba], [ba])
            return acc, ba

        def dt_phase(t0):
            slot, bs = wload("D", 0)
            for i in range(NTB):
                tile = t0 + i
                ps, pb = psum()
                for kc in range(KC):
                    MM(ps[:, 0:64], xnT[:, kc, i * 128:(i + 1) * 128], slot[:, kc * 64:(kc + 1) * 64], kc == 0, kc == KC - 1,
                       [b_xnT, bs], [pb])
                k = rotate("dtmp", 4)
                tm, btm = dtmp[:, k, :], b_dtmp[k]
                xr, ax, ex, ln_ = tm[:, 0:64], tm[:, 64:128], tm[:, 128:192], tm[:, 64:128]
                dt_i, dtA_i, od_i, w_i, cd_i = [dtt[:, i, q, :] for q in range(5)]
                bdt, bdtA, bod, bw, bcd = b_dt[i]
                TT("dve", xr, ps[:, 0:64], dtb_bc, ALU.add, [pb, b_rows], [btm])
                ACT(ax, xr, AF.Abs, [btm], [btm])
                ACT(ex, ax, AF.Exp, [btm], [btm], scale=-1.0)
                ACT(ln_, ex, AF.Ln, [btm, b_ccol], [btm], bias=one_c)
                TS("dve", xr, xr, 0.0, 0.0, ALU.max, ALU.add, [btm], [btm])
                TT("dve", xr, xr, ln_, ALU.add, [btm], [btm])
                TS("dve", dt_i, xr, maskt[:, tile:tile + 1], zero_c, ALU.mult, ALU.add, [btm, b_mask, b_ccol], [bdt])
                TT("dve", dtA_i, dt_i, Abc[:], ALU.mult, [bdt, b_A], [bdtA])
                ps3, pb3 = psum()
                MM(ps3[:, 0:64], triLE[:], dtA_i, True, True, [b_tri, bdtA], [pb3])
                MM(ps3[:, 64:128], U2[:], dtA_i, True, True, [b_U2, bdtA], [pb3])
                MM(ps3[:, 128:192], onesf[:], dtA_i, True, True, [b_ones, bdtA], [pb3])
                ACT(od_i, ps3[:, 0:64], AF.Exp, [pb3], [bod])
                ACT(ex, ps3[:, 64:128], AF.Exp, [pb3], [btm])
                ACT(cd_i, ps3[:, 128:192], AF.Exp, [pb3], [bcd])
                TT("dve", w_i, ex, dt_i, ALU.mult, [btm, bdt], [bw])

        def ssd_tile(g, i, full, gb):
            hs0 = g * 8
            xsB, bx = xsB_t[gb], b_xsB[gb]
            xs_t = xsB[:, i, 0:512]
            xs3 = v3(xs_t, 64)
            B_t = xsB[:, i, 512:640]
            dt_i, dtA_i, od_i, w_i, cd_i = [dtt[:, i, q, hs0:hs0 + 8] for q in range(5)]
            bdt, bdtA, bod, bw, bcd = b_dt[i]
            tok = slice(i * 128, (i + 1) * 128)
            Sg = S[:, g, :]
            if full:
                k = rotate("rs", 2)
                rs_, brs = rs_t[k], b_rs[k]
                TT("pool", rs_, bc3(dtA_i, 128, 2), bc3(triLE[:, :], 8, 1), ALU.mult, [bdtA, b_tri], [brs])
                psA, pbA = psum()
                psB, pbB = psum()
                MM(psA[:, :], U2[:], rs_[:, 0:4, :], True, True, [b_U2, brs], [pbA])
                MM(psB[:, :], U2[:], rs_[:, 4:8, :], True, True, [b_U2, brs], [pbB])
                k = rotate("E", 2)
                E, bE = E_t[k], b_E[k]
                ACT(E[:, 0:4, :], v3(psA[:, :], 128), AF.Exp, [pbA], [bE])
                ACT(E[:, 4:8, :], v3(psB[:, :], 128), AF.Exp, [pbB], [bE])
                psc, pbc = psum()
                MM(psc[:, 0:128], BT_t[gb][:, tok], CT_t[gb][:, tok], True, True, [b_BT[gb], b_CT[gb]], [pbc])
                k = rotate("cbm", 2)
                cbm, bcbm = cbm_t[k], b_cbm[k]
                TT("dve", cbm, psc[:, 0:128], triLE[:], ALU.mult, [pbc, b_tri], [bcbm])
                k = rotate("MT", 2)
                MT, bMT = MT_t[k], b_MT[k]
                TT("dve", MT, E, bc3(cbm, 8, 1), ALU.mult, [bE, bcbm], [bMT])
                k = rotate("xdt", 2)
                xdt, bxdt = xdt_t[k], b_xdt[k]
                TT("pool", v3(xdt, 64), xs3, bc3(dt_i, 64, 2), ALU.mult, [bx, bdt], [bxdt])
                k = rotate("Sb", 2)
                Sb, bSb = Sb_t[k], b_Sb[k]
                CP("pool", Sb, Sg, [b_S[g]], [bSb])
                yield
                psy, pby = psum()
                for r in range(8):
                    MM(psy[:, r * 64:(r + 1) * 64], MT[:, r, :], xdt[:, r * 64:(r + 1) * 64], True, True, [bMT, bxdt], [pby])
                pso, pbo = psum()
                MM(pso[:, :], CT_t[gb][:, tok], Sb, True, True, [b_CT[gb], bSb], [pbo])
                k = rotate("y", 4)
                y1, by1 = y_t[k], b_y[k]
                k = rotate("y", 4)
                y2, by2 = y_t[k], b_y[k]
                TT("dve", v3(y1, 64), v3(pso[:, :], 64), bc3(od_i, 64, 2), ALU.mult, [pbo, bod], [by1])
                TT("dve", y1, y1, psy[:, :], ALU.add, [by1, pby], [by1])
                TT("pool", v3(y2, 64), xs3, bc3(D_bc[:, hs0:hs0 + 8], 64, 2), ALU.mult, [bx, b_rows], [by2])
                TT("pool", y1, y1, y2, ALU.add, [by1, by2], [by1])
                TT("pool", y1, y1, sz_t[gb][:, i, :], ALU.mult, [by1, b_sz[gb]], [by1])
                ssq, bssq = sm()
                ACT(junk[:, 0:512], y1, AF.Square, [by1], [bssq], scale=float(512 ** -0.5), accum=ssq)
                rstd, brstd = rstd_from_ssq(ssq, bssq)
                ACT(y2, y1, AF.Copy, [by1, brstd], [by2], scale=rstd)
                k = rotate("si", 2)
                si, bsi = si_t[k], b_si[k]
                TT("pool", v3(si, 64), xs3, bc3(w_i, 64, 2), ALU.mult, [bx, bw], [bsi])
                yield
                pt, pbt = psum()
                for kk in range(4):
                    TR(pt[:, kk * 128:(kk + 1) * 128], y2[:, kk * 128:(kk + 1) * 128], identf[:], [by2, b_identf], [pbt])
                TT("dve", ymT[:, 4 * g:4 * g + 4, tok], v3(pt[:, :], 128),
                   bc3(cols[:, C_MBG + 4 * g:C_MBG + 4 * g + 4], 128, 2), ALU.mult, [pbt, b_cols], [b_ymT])
            if not full:
                k = rotate("si", 2)
                si, bsi = si_t[k], b_si[k]
                TT("pool", v3(si, 64), xs3, bc3(w_i, 64, 2), ALU.mult, [bx, bw], [bsi])
            psu, pbu = psum()
            MM(psu[:, :], B_t, si, True, True, [bx, bsi], [pbu])
            TT("pool", v3(Sg, 64), v3(Sg, 64), bc3(cd_i, 64, 2), ALU.mult, [b_S[g], bcd], [b_S[g]])
            TT("dve", Sg, Sg, psu[:, :], ALU.add, [b_S[g], pbu], [b_S[g]])
            yield

        def group_B(g, full):
            for i in range(NTB):
                yield from ssd_tile(g, i, full, g % 2)

        def group_phase(g, full):
            gb = g % 2
            xsB, bx = xsB_t[gb], b_xsB[gb]
            chunks = [("x", g * 4 + c, c) for c in range(4)] + [("B", 32 + g, 4)]
            if full:
                chunks.append(("C", 40 + g, 5))
            pending = None
            for kind, j, c in chunks:
                ps, pb = proj_S([S_XBC + j], xnT, b_xnT)

                def src_fn(dst, bm, ps=ps, pb=pb):
                    CP("act", dst, ps[:, 0:T], [pb], [bm])
                acc, ba = conv(src_fn, 4, cols[:, C_MBW + 4 * j:C_MBW + 4 * j + 4], cols[:, C_MBB + j:C_MBB + j + 1],
                               xbch[:, j, :], b_xbch[j])
                if kind == "C":
                    ACT(CT_t[gb], acc[:], AF.Silu, [ba], [b_CT[gb]])
                else:
                    if kind == "B":
                        fT, bfT = BT_t[gb], b_BT[gb]
                    else:
                        k = rotate("fT", 2)
                        fT, bfT = fT_t[k], b_fT[k]
                    ACT(fT, acc[:], AF.Silu, [ba], [bfT])
                if pending is not None:
                    pending()
                    pending = None
                if kind != "C":
                    def pending(fT=fT, bfT=bfT, c=c):
                        pt, pbt = psum()
                        ptb = v3(pt[:, :].bitcast(BF16)[:, 0:T], 128)
                        for i in range(NTB):
                            TR(ptb[:, i, :], fT[:, i * 128:(i + 1) * 128], identb[:], [bfT, b_identb], [pbt])
                        CP("dve", xsB[:, :, c * 128:(c + 1) * 128], ptb, [pbt], [bx])
                yield
            if pending is not None:
                pending()
                pending = None
            if full:
                zps = [psum() for _ in range(NTB)]
                for kg in range(4):
                    slot, bs = wload("M", M_Z + g * 4 + kg)
                    for i in range(NTB):
                        for kcl in range(4):
                            kc = kg * 4 + kcl
                            MM(zps[i][0][:, :], xnT[:, kc, i * 128:(i + 1) * 128], slot[:, kcl * 512:(kcl + 1) * 512],
                               kc == 0, kc == KC - 1, [b_xnT, bs], [zps[i][1]])
                for i in range(NTB):
                    ACT(sz_t[gb][:, i, :], zps[i][0][:, :], AF.Silu, [zps[i][1]], [b_sz[gb]])
            yield

        def sc_phase():
            for i in range(16):
                psc, pbc = proj_S([S_SC + 3 * i], xnT, b_xnT)
                psh, pbh = proj_S([S_SC + 3 * i + 1], xnT, b_xnT)
                psb, pbb = proj_S([S_SC + 3 * i + 2], xnT, b_xnT)
                ft, bft = ftbuf()
                CP("act", ft[:], psc[:, 0:T], [pbc], [bft])

                def src_fn(dst, bm, ft=ft, bft=bft, psh=psh, pbh=pbh):
                    TT("dve", dst, ft[:], psh[:, 0:T], ALU.mult, [bft, pbh], [bm])
                acc, ba = conv(src_fn, 3, cols[:, C_SCW + 3 * i:C_SCW + 3 * i + 3], None, sch[:, i, :], b_sch[i])
                TT("dve", yaT[:, i, :], acc[:], psb[:, 0:T], ALU.mult, [ba, pbb], [b_yaT])

        def mix_phase():
            for i in range(16):
                psa, pba = proj_S([S_WA + i], yaT, b_yaT)
                psm, pbm = proj_S([S_WM + 2 * i, S_WM + 2 * i + 1], ymT, b_ymT)
                pga, pbga = proj_S([S_GATE + i], xnT, b_xnT)
                pgm, pbgm = proj_S([S_GATE + 16 + i], xnT, b_xnT)
                ga, bga = ftbuf()
                gm, bgm = ftbuf()
                ACT(ga[:], pga[:, 0:T], AF.Sigmoid, [pbga, b_cols], [bga], bias=cols[:, C_BG + i:C_BG + i + 1])
                ACT(gm[:], pgm[:, 0:T], AF.Sigmoid, [pbgm, b_cols], [bgm], bias=cols[:, C_BG + 16 + i:C_BG + 16 + i + 1])
                TT("dve", ga[:], ga[:], psa[:, 0:T], ALU.mult, [bga, pba], [bga])
                TT("dve", gm[:], gm[:], psm[:, 0:T], ALU.mult, [bgm, pbm], [bgm])
                TT("pool", mixT[:, i, :], ga[:], gm[:], ALU.add, [bga, bgm], [b_mixT])

        def tm_proj(unit0, nkg, nkc, lhsT_T, blhs, evac):
            for q in range(4):
                banks = [psum() for _ in range(NTB)]
                for kg in range(nkg):
                    slot, bs = wload("M", unit0 + q * nkg + kg)
                    for i in range(NTB):
                        for kcl in range(4):
                            kc = kg * 4 + kcl
                            if kc >= nkc:
                                continue
                            MM(banks[i][0][:, :], lhsT_T[:, kc, i * 128:(i + 1) * 128], slot[:, kcl * 512:(kcl + 1) * 512],
                               kc == 0, kc == nkc - 1, [blhs, bs], [banks[i][1]])
                for i in range(NTB):
                    evac(q, i, banks[i][0], banks[i][1])

        def ffn_up_phase():
            for i in range(NFC):
                psu, pbu = proj_S([S_UP + 2 * i], xnT, b_xnT)
                psv, pbv = proj_S([S_UP + 2 * i + 1], xnT, b_xnT)

                def src_fn(dst, bm, psu=psu, pbu=pbu):
                    CP("act", dst, psu[:, 0:T], [pbu], [bm])
                acc, ba = conv(src_fn, 3, cols[:, C_FFW + 3 * i:C_FFW + 3 * i + 3], cols[:, C_FFB + i:C_FFB + i + 1],
                               uh[:, i, :], b_uh[i])
                ft, bft = ftbuf()
                ACT(ft[:], acc[:], AF.Silu, [ba], [bft])
                TT("dve", actT[:, i, :], ft[:], psv[:, 0:T], ALU.mult, [bft, pbv], [b_actT])

        nblk_p = NPFX // NTB
        nblk_f = NF // NTB
        for blk in range(nblk_p + nblk_f):
            full = blk >= nblk_p
            t0 = blk * NTB
            if full:
                pc_tick(len(pc_pending))
            switch_alias(R2B, R2A)
            switch_alias(R1B, R1A)
            for i in range(NTB):
                tile = t0 + i
                k = tile % 2
                P.op("sp", lambda e, k=k, tile=tile: e.dma_start(out=xld[k][:], in_=xin[tile * 128:(tile + 1) * 128, :]),
                     (), [b_xld[k]], dsem=d_x[k])
                norm_transpose(xld[k][:], b_xld[k], xld[k][:], b_xld[k], C_G1, xnT, b_xnT, i)
            dt_phase(t0)
            for _ in group_phase(0, full):
                pass
            for g in range(NG):
                gB = group_B(g, full)
                gA = group_phase(g + 1, full) if g + 1 < NG else iter(())
                doneA = doneB = False
                while not (doneA and doneB):
                    if not doneB:
                        try:
                            next(gB)
                        except StopIteration:
                            doneB = True
                    if not doneA:
                        try:
                            next(gA)
                        except StopIteration:
                            doneA = True
            if not full:
                pc_tick(-(-320 // max(1, nblk_p - 1)))
                continue
            sc_phase()
            switch_alias(R2A, R2B)
            mix_phase()
            for i in range(NTB):
                tile = t0 + i
                P.op("pool", lambda e, i=i, tile=tile: e.dma_start(out=h1[:, i, :], in_=xin[tile * 128:(tile + 1) * 128, :]),
                     (), [b_h1[i]], dsem=d_h[i])

            def evac_h(q, i, ps, pb):
                TT("dve", h1[:, i, q * 512:(q + 1) * 512], ps[:, :], h1[:, i, q * 512:(q + 1) * 512], ALU.add,
                   [pb, b_h1[i]], [b_h1[i]])
            tm_proj(M_WO, 4, KC, mixT, b_mixT, evac_h)
            for i in range(NTB):
                k = i % 2
                norm_transpose(h1[:, i, :], b_h1[i], xld[k][:], b_xld[k], C_G2, xnT, b_xnT, i)
            switch_alias(R1A, R1B)
            ffn_up_phase()
            tm_proj(M_DN, 11, NFC, actT, b_actT, evac_h)
            for i in range(NTB):
                ftile = (blk - nblk_p) * NTB + i
                if ftile == 0 or ftile > NT:
                    continue
                k = i % 2
                ssq, bssq = sm()
                ACT(junk[:], h1[:, i, :], AF.Square, [b_h1[i]], [bssq], scale=float(D ** -0.5), accum=ssq)
                rstd, brstd = rstd_from_ssq(ssq, bssq)
                STT("dve", xld[k][:], h1[:, i, :], rstd, gfbc[:], ALU.mult, ALU.mult, [b_h1[i], brstd, b_gf], [b_xld[k]])
                orow = (ftile - 1) * 128
                P.op("pool", lambda e, k=k, orow=orow: e.dma_start(out=outd[orow:orow + 128, :], in_=xld[k][:]),
                     [b_xld[k]], (), dsem=d_o[k])
        P.run(final_dsems=d_o)
        nc._sem_counts = dict(P.cnt)
    return nc


def _s_unit(W, col0, k0=0):
    blk = W[k0:k0 + 2048, col0:col0 + 128]
    return blk.reshape(16, 128, 128).transpose(1, 0, 2).reshape(128, 2048)


def _m_unit(W, col0, k0):
    K = W.shape[0]
    blk = np.zeros((512, 512), np.float32)
    n = max(0, min(512, K - k0))
    blk[:n] = W[k0:k0 + n, col0:col0 + 512]
    return blk.reshape(4, 128, 512).transpose(1, 0, 2).reshape(128, 2048)


def _colp(v):
    return np.ascontiguousarray(v.reshape(-1, 128).T)


def _convp(w):
    K, C = w.shape
    return np.ascontiguousarray(w.T.reshape(C // 128, 128, K).transpose(1, 0, 2).reshape(128, -1))


def prep_weights(norm1_g, w_in, b_gate, sc_conv_w, mb_conv_w, mb_conv_b, dt_bias, a_log, d_skip, mb_norm_g,
                 w_a, w_m, w_o, norm2_g, w_up, ffn_conv_w, ffn_conv_b, w_down, normf_g):
    f = lambda a: np.asarray(a, np.float32)
    w_in, w_a, w_m, w_o, w_up, w_down = f(w_in)[0], f(w_a)[0], f(w_m)[0], f(w_o)[0], f(w_up)[0], f(w_down)[0]
    WS = np.empty((NSU, 128, 2048), np.float32)
    WM = np.empty((NMU, 128, 2048), np.float32)
    o_scb, o_scc, o_sch, o_z, o_xbc, o_dt, o_gate = 0, 2048, 4096, 6144, 10240, 16384, 16448
    for i in range(16):
        WS[S_SC + 3 * i] = _s_unit(w_in, o_scc + i * 128)
        WS[S_SC + 3 * i + 1] = _s_unit(w_in, o_sch + i * 128)
        WS[S_SC + 3 * i + 2] = _s_unit(w_in, o_scb + i * 128)
    for j in range(48):
        WS[S_XBC + j] = _s_unit(w_in, o_xbc + j * 128)
    for j in range(32):
        WS[S_GATE + j] = _s_unit(w_in, o_gate + j * 128)
    for i in range(16):
        WS[S_WA + i] = _s_unit(w_a, i * 128)
        WS[S_WM + 2 * i] = _s_unit(w_m, i * 128, 0)
        WS[S_WM + 2 * i + 1] = _s_unit(w_m, i * 128, 2048)
    for i in range(NFC):
        WS[S_UP + 2 * i] = _s_unit(w_up, i * 128)
        WS[S_UP + 2 * i + 1] = _s_unit(w_up, DFF + i * 128)
    for g in range(8):
        for kg in range(4):
            WM[M_Z + g * 4 + kg] = _m_unit(w_in, o_z + g * 512, kg * 512)
    for q in range(4):
        for kg in range(4):
            WM[M_WO + q * 4 + kg] = _m_unit(w_o, q * 512, kg * 512)
        for kg in range(11):
            WM[M_DN + q * 11 + kg] = _m_unit(w_down, q * 512, kg * 512)
    WDT = np.ascontiguousarray(w_in[:, o_dt:o_dt + 64].reshape(16, 128, 64).transpose(1, 0, 2).reshape(128, 1024))
    cols = np.zeros((128, NCOLS), np.float32)
    cols[:, C_G1:C_G1 + 16] = _colp(f(norm1_g)[0])
    cols[:, C_G2:C_G2 + 16] = _colp(f(norm2_g)[0])
    cols[:, C_SCW:C_SCW + 48] = _convp(f(sc_conv_w)[0])
    cols[:, C_MBW:C_MBW + 192] = _convp(f(mb_conv_w)[0])
    cols[:, C_MBB:C_MBB + 48] = _colp(f(mb_conv_b)[0])
    cols[:, C_FFW:C_FFW + 129] = _convp(f(ffn_conv_w)[0])
    cols[:, C_FFB:C_FFB + 43] = _colp(f(ffn_conv_b)[0])
    cols[:, C_BG:C_BG + 32] = _colp(f(b_gate)[0])
    cols[:, C_MBG:C_MBG + 32] = _colp(f(mb_norm_g)[0])
    rows = np.concatenate([f(dt_bias)[0], f(a_log)[0], f(d_skip)[0]]).astype(np.float32)
    return {"WS": WS, "WM": WM, "WDT": WDT, "cols": cols, "rows": rows, "gf": np.ascontiguousarray(f(normf_g))}


def prep_streams(x, meta_tokens, NT, CPB, NTB):
    x = np.asarray(x, np.float32)
    meta = np.asarray(meta_tokens, np.float32)
    B = x.shape[0]
    NPFX, NF = _rup((CPB - 1) * NT, NTB), _rup(NT + 1, NTB)
    NTILES = NPFX + NF
    NEND = NPFX + NT + 1
    streams, masks = [], []
    for b in range(B):
        seq = np.zeros(((CPB * NT + 1) * 128, D), np.float32)
        seq[128 - N_META:128] = meta
        seq[128:] = x[b]
        valid = np.ones(((CPB * NT + 1) * 128,), np.float32)
        valid[:128 - N_META] = 0.0
        for j in range(CPB):
            n_real = (j + 1) * NT + 1
            s = np.zeros((NTILES * 128, D), np.float32)
            m = np.zeros((NTILES * 128,), np.float32)
            s[(NEND - n_real) * 128:NEND * 128] = seq[:n_real * 128]
            m[(NEND - n_real) * 128:NEND * 128] = valid[:n_real * 128]
            streams.append(s)
            masks.append(np.ascontiguousarray(m.reshape(NTILES, 128).T))
    return streams, masks


_CACHE = {}


def run(inputs, NT, NTB, CPB):
    key = (NT, NTB, CPB)
    if key not in _CACHE:
        _CACHE[key] = build(NT, NTB, CPB)
    nc = _CACHE[key]
    wd = prep_weights(**{k: v for k, v in inputs.items() if k not in ("x", "meta_tokens")})
    streams, masks = prep_streams(inputs["x"], inputs["meta_tokens"], NT, CPB, NTB)
    B = 2
    ncores = B * CPB
    in_maps = []
    for c in range(ncores):
        m = dict(wd)
        m["xin"] = streams[c]
        m["mask"] = masks[c]
        in_maps.append(m)
    res = run_bass_kernel_spmd(nc, in_maps, core_ids=list(range(ncores)))
    out = np.empty((B, CPB * NT * 128, D), np.float32)
    for c in range(ncores):
        b, j = divmod(c, CPB)
        out[b, j * NT * 128:(j + 1) * NT * 128] = res.results[c]["out"]
    return out


CPB_FULL = 2


def kernel(**inputs):
    return run(inputs, SEQ // 128 // CPB_FULL, 3, CPB_FULL)
```

```python
import numpy as np
from contextlib import ExitStack
import concourse.bass as bass
import concourse.mybir as mybir
from concourse.bass_utils import run_bass_kernel_spmd

F32 = mybir.dt.float32
BF16 = mybir.dt.bfloat16
AF = mybir.ActivationFunctionType
ALU = mybir.AluOpType

D = 2048
KC = 16
DI = 4096
NG = 8
NH = 64
DFF = 5504
NFC = 43
EPS = 1e-6
N_META = 16
SEQ = 16384
BATCH = 2

S_SC, S_XBC, S_GATE, S_WA, S_WM, S_UP, NSU = 0, 48, 96, 128, 144, 176, 262
M_Z, M_WO, M_DN, NMU = 0, 32, 48, 92
C_G1, C_G2, C_SCW, C_MBW, C_MBB, C_FFW, C_FFB, C_BG, C_MBG, NCOLS = 0, 16, 32, 80, 272, 320, 449, 492, 524, 556

ENGS = ("pe", "act", "dve", "pool", "sp")
INORDER = ("pe", "act", "dve", "pool")


class Buf:
    __slots__ = ("name", "w", "r")

    def __init__(self, name):
        self.name = name
        self.w = None
        self.r = {}


class DSem:
    __slots__ = ("h", "count")

    def __init__(self, h):
        self.h = h
        self.count = 0


class Rec:
    __slots__ = ("eng", "fn", "deps", "need", "sem", "val", "dma", "idx")


class Prog:
    def __init__(self, nc, stack):
        self.nc = nc
        self.ops = {e: [] for e in ENGS}
        self.esem = {}
        for e in INORDER:
            self.esem[e] = stack.enter_context(nc.semaphore("s_" + e))
        self.stack = stack
        self.nrec = 0

    def dsem(self, name):
        return DSem(self.stack.enter_context(self.nc.semaphore(name)))

    def op(self, eng, fn, reads=(), writes=(), dsem=None):
        rec = Rec()
        rec.eng = eng
        rec.fn = fn
        rec.need = dsem is not None
        rec.dma = dsem
        rec.sem = None
        rec.val = 0
        rec.idx = self.nrec
        self.nrec += 1
        deps = {}
        wr = {}
        for b in reads:
            if b.w is not None:
                deps[id(b.w)] = b.w
                wr[id(b.w)] = True
        for b in writes:
            if b.w is not None:
                deps[id(b.w)] = b.w
                wr[id(b.w)] = True
            for r in b.r.values():
                deps[id(r)] = r
        out = []
        isdma = dsem is not None
        for k, d in deps.items():
            if d is rec:
                continue
            if d.dma is None and not isdma and d.eng == eng:
                if eng == "pe":
                    continue
                if k not in wr:
                    continue
            d.need = True
            out.append(d)
        rec.deps = out
        key = ("dma", rec.idx) if isdma else eng
        for b in reads:
            b.r[key] = rec
        for b in writes:
            b.w = rec
            b.r = {}
        self.ops[eng].append(rec)
        return rec

    def finalize(self):
        cnt = {e: 0 for e in INORDER}
        for e in ENGS:
            for rec in self.ops[e]:
                if rec.dma is not None:
                    rec.dma.count += 16
                    rec.sem = rec.dma.h
                    rec.val = rec.dma.count
                elif rec.need:
                    cnt[e] += 1
                    rec.sem = self.esem[e]
                    rec.val = cnt[e]
        self.cnt = cnt

    def emit_engine(self, e, eng, final_dsems=()):
        waited = {}
        for rec in self.ops[e]:
            for d in rec.deps:
                k = id(d.sem)
                if waited.get(k, 0) >= d.val:
                    continue
                waited[k] = d.val
                eng.wait_ge(d.sem, d.val)
            ins = rec.fn(eng)
            if rec.dma is not None:
                ins.then_inc(rec.sem, 16)
            elif rec.need:
                ins.then_inc(rec.sem, 1)
        for ds in final_dsems:
            if ds.count:
                eng.wait_ge(ds.h, ds.count)

    def run(self, final_dsems=()):
        self.finalize()
        nc = self.nc
        with nc.Block() as block:
            @block.tensor
            def _(e):
                self.emit_engine("pe", e)

            @block.scalar
            def _(e):
                self.emit_engine("act", e)

            @block.vector
            def _(e):
                self.emit_engine("dve", e)

            @block.gpsimd
            def _(e):
                self.emit_engine("pool", e, final_dsems)

            @block.sync
            def _(e):
                self.emit_engine("sp", e)


def switch_alias(old_bufs, new_bufs):
    accs = {}
    for b in old_bufs:
        if b.w is not None:
            accs[id(b.w)] = b.w
        for r in b.r.values():
            accs[id(r)] = r
    for n in new_bufs:
        n.w = None
        n.r = {("al", k): rec for k, rec in accs.items()}


def _rup(a, b):
    return -(-a // b) * b


def build(NT, NTB, CPB=4, NSLOT=8):
    NF = _rup(NT + 1, NTB)
    NPFX = _rup((CPB - 1) * NT, NTB)
    NTILES = NPFX + NF
    assert NF % NTB == 0 and NPFX % NTB == 0
    T = NTB * 128
    nc = bass.Bass("TRN2", target_bir_lowering=False)
    xin = nc.dram_tensor("xin", [NTILES * 128, D], F32, kind="ExternalInput").ap()
    maskd = nc.dram_tensor("mask", [128, NTILES], F32, kind="ExternalInput").ap()
    WS = nc.dram_tensor("WS", [NSU, 128, 2048], F32, kind="ExternalInput").ap()
    WM = nc.dram_tensor("WM", [NMU, 128, 2048], F32, kind="ExternalInput").ap()
    WDT = nc.dram_tensor("WDT", [128, 1024], F32, kind="ExternalInput").ap()
    colsd = nc.dram_tensor("cols", [128, NCOLS], F32, kind="ExternalInput").ap()
    rowsd = nc.dram_tensor("rows", [192], F32, kind="ExternalInput").ap()
    gfd = nc.dram_tensor("gf", [D], F32, kind="ExternalInput").ap()
    outd = nc.dram_tensor("out", [NT * 128, D], F32, kind="ExternalOutput").ap()
    WSb = nc.dram_tensor("WSb", [NSU, 128, 2048], BF16, kind="Internal").ap()
    WMb = nc.dram_tensor("WMb", [NMU, 128, 2048], BF16, kind="Internal").ap()
    WDTb = nc.dram_tensor("WDTb", [128, 1024], BF16, kind="Internal").ap()

    with ExitStack() as st:
        P = Prog(nc, st)

        def sb(name, shape, dt):
            return st.enter_context(nc.sbuf_tensor(name, shape, dt))

        def MM(out, lhsT, rhs, start, stop, r, w):
            P.op("pe", lambda e: e.matmul(out, lhsT=lhsT, rhs=rhs, start=start, stop=stop), r, w)

        def TR(out, in_, ident, r, w):
            P.op("pe", lambda e: e.transpose(out=out, in_=in_, identity=ident), r, w)

        def ACT(out, in_, func, r, w, bias=None, scale=None, accum=None):
            kw = {}
            if bias is not None:
                kw["bias"] = bias
            if scale is not None:
                kw["scale"] = scale
            if accum is not None:
                kw["accum_out"] = accum
            P.op("act", lambda e: e.activation(out=out, in_=in_, func=func, **kw), r, w)

        def TT(eng, out, in0, in1, op, r, w):
            P.op(eng, lambda e: e.tensor_tensor(out=out, in0=in0, in1=in1, op=op), r, w)

        def TS(eng, out, in0, s1, s2, op0, op1, r, w):
            P.op(eng, lambda e: e.tensor_scalar(out=out, in0=in0, scalar1=s1, scalar2=s2, op0=op0, op1=op1), r, w)

        def STT(eng, out, in0, scalar, in1, op0, op1, r, w):
            P.op(eng, lambda e: e.scalar_tensor_tensor(out=out, in0=in0, scalar=scalar, in1=in1, op0=op0, op1=op1), r, w)

        def CP(eng, out, in_, r, w):
            if eng == "act":
                P.op("act", lambda e: e.activation(out=out, in_=in_, func=AF.Copy), r, w)
            else:
                P.op(eng, lambda e: e.tensor_copy(out=out, in_=in_), r, w)

        def MSET(eng, ap, val, w):
            P.op(eng, lambda e: e.memset(ap, val), (), w)

        def bc3(ap2, n, pos):
            m = ap2.shape[1]
            if pos == 2:
                return ap2.unsqueeze(2).to_broadcast([128, m, n])
            return ap2.unsqueeze(1).to_broadcast([128, n, m])

        def v3(ap2, b):
            return ap2.rearrange("p (a b) -> p a b", b=b)

        identf = sb("identf", [128, 128], F32); b_identf = Buf("identf")
        identb = sb("identb", [128, 128], BF16); b_identb = Buf("identb")
        triLE = sb("triLE", [128, 128], F32); b_tri = Buf("tri")
        U2 = sb("U2", [128, 128], F32); b_U2 = Buf("U2")
        onesf = sb("onesf", [128, 128], F32); b_ones = Buf("ones")
        cols = sb("cols_sb", [128, NCOLS], F32); b_cols = Buf("cols")
        rows = sb("rows_sb", [128, 192], F32); b_rows = Buf("rows")
        Abc = sb("Abc", [128, 64], F32); b_A = Buf("A")
        gfbc = sb("gfbc", [128, D], F32); b_gf = Buf("gf")
        maskt = sb("maskt", [128, NTILES], F32); b_mask = Buf("mask")
        ccol = sb("ccol", [128, 3], F32); b_ccol = Buf("ccol")
        junk = sb("junk", [128, D], BF16)
        dtb_bc = rows[:, 0:64]
        D_bc = rows[:, 128:192]

        d_c = [P.dsem("d_c%d" % k) for k in range(4)]
        P.op("sp", lambda e: e.dma_start(out=cols[:], in_=colsd[:, :]), (), [b_cols], dsem=d_c[0])
        P.op("sp", lambda e: e.dma_start(out=rows[:], in_=rowsd.partition_broadcast(128)), (), [b_rows], dsem=d_c[1])
        P.op("sp", lambda e: e.dma_start(out=gfbc[:], in_=gfd.partition_broadcast(128)), (), [b_gf], dsem=d_c[2])
        P.op("sp", lambda e: e.dma_start(out=maskt[:], in_=maskd[:, :]), (), [b_mask], dsem=d_c[3])

        MSET("pool", identf[:], 1.0, [b_identf])
        P.op("pool", lambda e: e.affine_select(out=identf[:], in_=identf[:], pattern=[[-1, 128]], compare_op=ALU.is_equal,
                                               fill=0.0, base=0, channel_multiplier=1), [b_identf], [b_identf])
        CP("dve", identb[:], identf[:], [b_identf], [b_identb])
        MSET("pool", triLE[:], 1.0, [b_tri])
        P.op("pool", lambda e: e.affine_select(out=triLE[:], in_=triLE[:], pattern=[[1, 128]], compare_op=ALU.is_ge,
                                               fill=0.0, base=0, channel_multiplier=-1), [b_tri], [b_tri])
        MSET("pool", U2[:], 1.0, [b_U2])
        P.op("pool", lambda e: e.affine_select(out=U2[:], in_=U2[:], pattern=[[-1, 128]], compare_op=ALU.is_gt,
                                               fill=0.0, base=0, channel_multiplier=1), [b_U2], [b_U2])
        MSET("dve", onesf[:], 1.0, [b_ones])
        MSET("dve", ccol[:, 0:1], 1.0, [b_ccol])
        MSET("dve", ccol[:, 1:2], EPS, [b_ccol])
        MSET("dve", ccol[:, 2:3], 0.0, [b_ccol])
        one_c = ccol[:, 0:1]
        eps_c = ccol[:, 1:2]
        zero_c = ccol[:, 2:3]
        ACT(Abc[:], rows[:, 64:128], AF.Exp, [b_rows], [b_A])
        TS("dve", Abc[:], Abc[:], -1.0, 0.0, ALU.mult, ALU.add, [b_A], [b_A])

        S = sb("S", [128, NG, 512], F32); b_S = [Buf("S%d" % g) for g in range(NG)]
        xbch = sb("xbch", [128, 48, 3], F32); b_xbch = [Buf("xbch%d" % j) for j in range(48)]
        sch = sb("sch", [128, 16, 2], F32); b_sch = [Buf("sch%d" % j) for j in range(16)]
        uh = sb("uh", [128, NFC, 2], F32); b_uh = [Buf("uh%d" % j) for j in range(NFC)]
        MSET("pool", S[:], 0.0, b_S)
        MSET("pool", xbch[:], 0.0, b_xbch)
        MSET("pool", sch[:], 0.0, b_sch)
        MSET("pool", uh[:], 0.0, b_uh)

        NSM = 64
        smt = sb("smt", [128, NSM], F32); b_sm = [Buf("sm%d" % k) for k in range(NSM)]
        sm_pos = [0]

        def sm():
            k = sm_pos[0] % NSM
            sm_pos[0] += 1
            return smt[:, k:k + 1], b_sm[k]

        dtt = sb("dtt", [128, NTB, 5, 64], F32)
        b_dt = [[Buf("dt%d_%d" % (i, k)) for k in range(5)] for i in range(NTB)]
        dtmp = sb("dtmp", [128, 4, 192], F32); b_dtmp = [Buf("dtmp%d" % k) for k in range(4)]

        xnT = sb("xnT", [128, KC, T], BF16); b_xnT = Buf("xnT")
        xld = [sb("xld%d" % k, [128, D], F32) for k in range(2)]; b_xld = [Buf("xld0"), Buf("xld1")]
        d_x = [P.dsem("d_x0"), P.dsem("d_x1")]
        d_o = [P.dsem("d_o0"), P.dsem("d_o1")]
        d_h = [P.dsem("d_h%d" % i) for i in range(NTB)]

        NRAW = 3
        rawt = [sb("raw%d" % k, [128, T + 3], F32) for k in range(NRAW)]
        b_rawh = [Buf("rawh%d" % k) for k in range(NRAW)]
        b_rawm = [Buf("rawm%d" % k) for k in range(NRAW)]
        acct = [sb("acc%d" % k, [128, T], F32) for k in range(NRAW)]
        b_acc = [Buf("acc%d" % k) for k in range(NRAW)]
        raw_pos = [0]
        ftmp = [sb("ftmp%d" % k, [128, T], F32) for k in range(2)]; b_ftmp = [Buf("ftmp0"), Buf("ftmp1")]
        ft_pos = [0]

        def ftbuf():
            k = ft_pos[0] % 2
            ft_pos[0] += 1
            return ftmp[k], b_ftmp[k]

        R1 = sb("R1", [128, 48 * T], BF16)
        ymT = v3(R1[:, 0:32 * T], T); b_ymT = Buf("ymT")
        yaT = v3(R1[:, 32 * T:48 * T], T); b_yaT = Buf("yaT")
        actT = v3(R1[:, 0:44 * T], T); b_actT = Buf("actT")

        off = [0]
        r2items = []

        def r2(nel_bf16):
            o = off[0]
            off[0] += nel_bf16
            return o

        o_xsB = [r2(NTB * 640) for _ in range(2)]
        o_BT = [r2(T) for _ in range(2)]
        o_CT = [r2(T) for _ in range(2)]
        o_sz = [r2(NTB * 512) for _ in range(2)]
        o_fT = [r2(T) for _ in range(2)]
        o_rs = [r2(2048) for _ in range(2)]
        o_E = [r2(1024) for _ in range(2)]
        o_MT = [r2(1024) for _ in range(2)]
        o_cbm = [r2(128) for _ in range(2)]
        o_xdt = [r2(512) for _ in range(2)]
        o_si = [r2(512) for _ in range(2)]
        o_Sb = [r2(512) for _ in range(2)]
        o_y = [r2(1024) for _ in range(4)]
        szA = off[0]
        szB = 16 * T + NTB * D * 2
        R2 = sb("R2", [128, max(szA, szB)], BF16)

        def r2v(o, n, dt=BF16):
            ap = R2[:, o:o + n]
            return ap.bitcast(F32) if dt == F32 else ap

        xsB_t = [v3(r2v(o, NTB * 640), 640) for o in o_xsB]; b_xsB = [Buf("xsB0"), Buf("xsB1")]
        BT_t = [r2v(o, T) for o in o_BT]; b_BT = [Buf("BT0"), Buf("BT1")]
        CT_t = [r2v(o, T) for o in o_CT]; b_CT = [Buf("CT0"), Buf("CT1")]
        sz_t = [v3(r2v(o, NTB * 512), 512) for o in o_sz]; b_sz = [Buf("sz0"), Buf("sz1")]
        fT_t = [r2v(o, T) for o in o_fT]; b_fT = [Buf("fT0"), Buf("fT1")]
        rs_t = [v3(r2v(o, 2048, F32), 128) for o in o_rs]; b_rs = [Buf("rs0"), Buf("rs1")]
        E_t = [v3(r2v(o, 1024), 128) for o in o_E]; b_E = [Buf("E0"), Buf("E1")]
        MT_t = [v3(r2v(o, 1024), 128) for o in o_MT]; b_MT = [Buf("MT0"), Buf("MT1")]
        cbm_t = [r2v(o, 128) for o in o_cbm]; b_cbm = [Buf("cbm0"), Buf("cbm1")]
        xdt_t = [r2v(o, 512) for o in o_xdt]; b_xdt = [Buf("xdt0"), Buf("xdt1")]
        si_t = [r2v(o, 512) for o in o_si]; b_si = [Buf("si0"), Buf("si1")]
        Sb_t = [r2v(o, 512) for o in o_Sb]; b_Sb = [Buf("Sb0"), Buf("Sb1")]
        y_t = [r2v(o, 1024, F32) for o in o_y]; b_y = [Buf("y%d" % k) for k in range(4)]
        R2A = b_xsB + b_BT + b_CT + b_sz + b_fT + b_rs + b_E + b_MT + b_cbm + b_xdt + b_si + b_Sb + b_y
        mixT = v3(R2[:, 0:16 * T], T); b_mixT = Buf("mixT")
        h1 = v3(R2[:, 16 * T:16 * T + NTB * D * 2].bitcast(F32), D); b_h1 = [Buf("h1_%d" % i) for i in range(NTB)]
        R2B = [b_mixT] + b_h1
        R1A = [b_ymT, b_yaT]
        R1B = [b_actT]
        rot = {}

        def rotate(name, n):
            k = rot.get(name, 0)
            rot[name] = k + 1
            return k % n

        pst = [st.enter_context(nc.psum_tensor("ps%d" % k, [128, 512], F32)) for k in range(8)]
        b_ps = [Buf("ps%d" % k) for k in range(8)]
        ps_pos = [0]

        def psum():
            k = ps_pos[0] % 8
            ps_pos[0] += 1
            return pst[k], b_ps[k]

        ring = [sb("ring%d" % s, [128, 2048], BF16) for s in range(NSLOT)]
        b_ring = [Buf("ring%d" % s) for s in range(NSLOT)]
        d_ring = [P.dsem("d_ring%d" % s) for s in range(NSLOT)]
        ring_pos = [0]
        NPC = 8
        d_pc = [P.dsem("d_pc%d" % k) for k in range(NPC)]
        b_pcs = [Buf("pcs%d" % k) for k in range(NPC)]
        b_wd = {}
        pc_pos = [0]

        pc_pending = []

        def pc_tick(n):
            for _ in range(n):
                if pc_pending:
                    precast_now(*pc_pending.pop(0))

        def precast(kind, idx, now=False):
            if now:
                precast_now(kind, idx)
            else:
                pc_pending.append((kind, idx))

        def precast_now(kind, idx):
            key = (kind, idx)
            if key in b_wd:
                return
            b = Buf("wd_%s%d" % (kind, idx))
            b_wd[key] = b
            k = pc_pos[0] % NPC
            pc_pos[0] += 1
            if kind == "S":
                src, dst = WS[idx], WSb[idx]
            elif kind == "M":
                src, dst = WM[idx], WMb[idx]
            else:
                src, dst = WDT[:, :], WDTb[:, :]
            P.op("pool", lambda e: e.dma_start(out=dst, in_=src), (), [b, b_pcs[k]], dsem=d_pc[k])

        def wload(kind, idx):
            s = ring_pos[0] % NSLOT
            ring_pos[0] += 1
            if kind == "S":
                src, n = WSb[idx], 2048
            elif kind == "M":
                src, n = WMb[idx], 2048
            else:
                src, n = WDTb[:, :], 1024
            dst = ring[s][:, 0:n]
            P.op("sp", lambda e: e.dma_start(out=dst, in_=src), [b_wd[(kind, idx)]], [b_ring[s]], dsem=d_ring[s])
            return ring[s], b_ring[s]

        precast("D", 0, True)
        for g in range(NG):
            for c in range(4):
                precast("S", S_XBC + g * 4 + c, True)
            precast("S", S_XBC + 32 + g, True)
        for g in range(NG):
            precast("S", S_XBC + 40 + g)
            for kg in range(4):
                precast("M", M_Z + g * 4 + kg)
        for i in range(48):
            precast("S", S_SC + i)
        for i in range(16):
            precast("S", S_WA + i)
            precast("S", S_WM + 2 * i)
            precast("S", S_WM + 2 * i + 1)
            precast("S", S_GATE + i)
            precast("S", S_GATE + 16 + i)
        for i in range(16):
            precast("M", M_WO + i)
        for i in range(86):
            precast("S", S_UP + i)
        for i in range(44):
            precast("M", M_DN + i)

        def rstd_from_ssq(ssq, bssq):
            lnv, bl = sm()
            ACT(lnv, ssq, AF.Ln, [bssq, b_ccol], [bl], bias=eps_c)
            rs_, br = sm()
            ACT(rs_, lnv, AF.Exp, [bl], [br], scale=-0.5)
            return rs_, br

        def norm_transpose(src, bsrc, dst_tm, bdst, gcol_off, dstT, bdstT, i):
            ssq, bssq = sm()
            ACT(junk[:], src, AF.Square, [bsrc], [bssq], scale=float(D ** -0.5), accum=ssq)
            rs_, br = rstd_from_ssq(ssq, bssq)
            ACT(dst_tm, src, AF.Copy, [bsrc, br], [bdst], scale=rs_)
            for q in range(4):
                ps, pb = psum()
                for kk in range(4):
                    c = 4 * q + kk
                    TR(ps[:, kk * 128:(kk + 1) * 128], dst_tm[:, c * 128:(c + 1) * 128], identf[:], [bdst, b_identf], [pb])
                TT("dve", dstT[:, 4 * q:4 * q + 4, i * 128:(i + 1) * 128], v3(ps[:, :], 128),
                   bc3(cols[:, gcol_off + 4 * q:gcol_off + 4 * q + 4], 128, 2), ALU.mult, [pb, b_cols], [bdstT])

        def proj_S(units, rhsT, brhs):
            ps, pb = psum()
            nk = len(units) * KC
            k = 0
            for ui, u in enumerate(units):
                slot, bs = wload("S", u)
                for kc in range(KC):
                    MM(ps[:, 0:T], slot[:, kc * 128:(kc + 1) * 128], rhsT[:, ui * KC + kc, :], k == 0, k == nk - 1,
                       [bs, brhs], [pb])
                    k += 1
            return ps, pb

        def conv(src_fn, K, wcol, bias, halo, bhalo):
            k = raw_pos[0] % NRAW
            raw_pos[0] += 1
            raw, bh, bm = rawt[k], b_rawh[k], b_rawm[k]
            acc, ba = acct[k], b_acc[k]
            CP("pool", raw[:, 0:K - 1], halo, [bhalo], [bh])
            src_fn(raw[:, K - 1:K - 1 + T], bm)
            CP("pool", halo, raw[:, T:T + K - 1], [bm], [bhalo])
            TS("dve", acc[:], raw[:, K - 1:K - 1 + T], wcol[:, K - 1:K], bias if bias is not None else zero_c,
               ALU.mult, ALU.add, [bm, b_cols, b_ccol], [ba])
            for kk in range(K - 1):
                STT("dve", acc[:], raw[:, kk:kk + T], wcol[:, kk:kk + 1], acc[:], ALU.mult, ALU.add,
                    [bm, bh, b_cols, ba], [ba])
            return acc, ba

        def dt_phase(t0):
            slot, bs = wload("D", 0)
            for i in range(NTB):
                tile = t0 + i
                ps, pb = psum()
                for kc in range(KC):
                    MM(ps[:, 0:64], xnT[:, kc, i * 128:(i + 1) * 128], slot[:, kc * 64:(kc + 1) * 64], kc == 0, kc == KC - 1,
                       [b_xnT, bs], [pb])
                k = rotate("dtmp", 4)
                tm, btm = dtmp[:, k, :], b_dtmp[k]
                xr, ax, ex, ln_ = tm[:, 0:64], tm[:, 64:128], tm[:, 128:192], tm[:, 64:128]
                dt_i, dtA_i, od_i, w_i, cd_i = [dtt[:, i, q, :] for q in range(5)]
                bdt, bdtA, bod, bw, bcd = b_dt[i]
                TT("dve", xr, ps[:, 0:64], dtb_bc, ALU.add, [pb, b_rows], [btm])
                ACT(ax, xr, AF.Abs, [btm], [btm])
                ACT(ex, ax, AF.Exp, [btm], [btm], scale=-1.0)
                ACT(ln_, ex, AF.Ln, [btm, b_ccol], [btm], bias=one_c)
                TS("dve", xr, xr, 0.0, 0.0, ALU.max, ALU.add, [btm], [btm])
                TT("dve", xr, xr, ln_, ALU.add, [btm], [btm])
                TS("dve", dt_i, xr, maskt[:, tile:tile + 1], zero_c, ALU.mult, ALU.add, [btm, b_mask, b_ccol], [bdt])
                TT("dve", dtA_i, dt_i, Abc[:], ALU.mult, [bdt, b_A], [bdtA])
                ps3, pb3 = psum()
                MM(ps3[:, 0:64], triLE[:], dtA_i, True, True, [b_tri, bdtA], [pb3])
                MM(ps3[:, 64:128], U2[:], dtA_i, True, True, [b_U2, bdtA], [pb3])
                MM(ps3[:, 128:192], onesf[:], dtA_i, True, True, [b_ones, bdtA], [pb3])
                ACT(od_i, ps3[:, 0:64], AF.Exp, [pb3], [bod])
                ACT(ex, ps3[:, 64:128], AF.Exp, [pb3], [btm])
                ACT(cd_i, ps3[:, 128:192], AF.Exp, [pb3], [bcd])
                TT("dve", w_i, ex, dt_i, ALU.mult, [btm, bdt], [bw])

        def ssd_tile(g, i, full, gb):
            hs0 = g * 8
            xsB, bx = xsB_t[gb], b_xsB[gb]
            xs_t = xsB[:, i, 0:512]
            xs3 = v3(xs_t, 64)
            B_t = xsB[:, i, 512:640]
            dt_i, dtA_i, od_i, w_i, cd_i = [dtt[:, i, q, hs0:hs0 + 8] for q in range(5)]
            bdt, bdtA, bod, bw, bcd = b_dt[i]
            tok = slice(i * 128, (i + 1) * 128)
            Sg = S[:, g, :]
            if full:
                k = rotate("rs", 2)
                rs_, brs = rs_t[k], b_rs[k]
                TT("pool", rs_, bc3(dtA_i, 128, 2), bc3(triLE[:, :], 8, 1), ALU.mult, [bdtA, b_tri], [brs])
                psA, pbA = psum()
                psB, pbB = psum()
                MM(psA[:, :], U2[:], rs_[:, 0:4, :], True, True, [b_U2, brs], [pbA])
                MM(psB[:, :], U2[:], rs_[:, 4:8, :], True, True, [b_U2, brs], [pbB])
                k = rotate("E", 2)
                E, bE = E_t[k], b_E[k]
                ACT(E[:, 0:4, :], v3(psA[:, :], 128), AF.Exp, [pbA], [bE])
                ACT(E[:, 4:8, :], v3(psB[:, :], 128), AF.Exp, [pbB], [bE])
                psc, pbc = psum()
                MM(psc[:, 0:128], BT_t[gb][:, tok], CT_t[gb][:, tok], True, True, [b_BT[gb], b_CT[gb]], [pbc])
                k = rotate("cbm", 2)
                cbm, bcbm = cbm_t[k], b_cbm[k]
                TT("dve", cbm, psc[:, 0:128], triLE[:], ALU.mult, [pbc, b_tri], [bcbm])
                k = rotate("MT", 2)
                MT, bMT = MT_t[k], b_MT[k]
                TT("dve", MT, E, bc3(cbm, 8, 1), ALU.mult, [bE, bcbm], [bMT])
                k = rotate("xdt", 2)
                xdt, bxdt = xdt_t[k], b_xdt[k]
                TT("pool", v3(xdt, 64), xs3, bc3(dt_i, 64, 2), ALU.mult, [bx, bdt], [bxdt])
                k = rotate("Sb", 2)
                Sb, bSb = Sb_t[k], b_Sb[k]
                CP("pool", Sb, Sg, [b_S[g]], [bSb])
                yield
                psy, pby = psum()
                for r in range(8):
                    MM(psy[:, r * 64:(r + 1) * 64], MT[:, r, :], xdt[:, r * 64:(r + 1) * 64], True, True, [bMT, bxdt], [pby])
                pso, pbo = psum()
                MM(pso[:, :], CT_t[gb][:, tok], Sb, True, True, [b_CT[gb], bSb], [pbo])
                k = rotate("y", 4)
                y1, by1 = y_t[k], b_y[k]
                k = rotate("y", 4)
                y2, by2 = y_t[k], b_y[k]
                TT("dve", v3(y1, 64), v3(pso[:, :], 64), bc3(od_i, 64, 2), ALU.mult, [pbo, bod], [by1])
                TT("dve", y1, y1, psy[:, :], ALU.add, [by1, pby], [by1])
                TT("pool", v3(y2, 64), xs3, bc3(D_bc[:, hs0:hs0 + 8], 64, 2), ALU.mult, [bx, b_rows], [by2])
                TT("pool", y1, y1, y2, ALU.add, [by1, by2], [by1])
                TT("pool", y1, y1, sz_t[gb][:, i, :], ALU.mult, [by1, b_sz[gb]], [by1])
                ssq, bssq = sm()
                ACT(junk[:, 0:512], y1, AF.Square, [by1], [bssq], scale=float(512 ** -0.5), accum=ssq)
                rstd, brstd = rstd_from_ssq(ssq, bssq)
                ACT(y2, y1, AF.Copy, [by1, brstd], [by2], scale=rstd)
                k = rotate("si", 2)
                si, bsi = si_t[k], b_si[k]
                TT("pool", v3(si, 64), xs3, bc3(w_i, 64, 2), ALU.mult, [bx, bw], [bsi])
                yield
                pt, pbt = psum()
                for kk in range(4):
                    TR(pt[:, kk * 128:(kk + 1) * 128], y2[:, kk * 128:(kk + 1) * 128], identf[:], [by2, b_identf], [pbt])
                TT("dve", ymT[:, 4 * g:4 * g + 4, tok], v3(pt[:, :], 128),
                   bc3(cols[:, C_MBG + 4 * g:C_MBG + 4 * g + 4], 128, 2), ALU.mult, [pbt, b_cols], [b_ymT])
            if not full:
                k = rotate("si", 2)
                si, bsi = si_t[k], b_si[k]
                TT("pool", v3(si, 64), xs3, bc3(w_i, 64, 2), ALU.mult, [bx, bw], [bsi])
            psu, pbu = psum()
            MM(psu[:, :], B_t, si, True, True, [bx, bsi], [pbu])
            TT("pool", v3(Sg, 64), v3(Sg, 64), bc3(cd_i, 64, 2), ALU.mult, [b_S[g], bcd], [b_S[g]])
            TT("dve", Sg, Sg, psu[:, :], ALU.add, [b_S[g], pbu], [b_S[g]])
            yield

        def group_B(g, full):
            for i in range(NTB):
                yield from ssd_tile(g, i, full, g % 2)

        def group_phase(g, full):
            gb = g % 2
            xsB, bx = xsB_t[gb], b_xsB[gb]
            chunks = [("x", g * 4 + c, c) for c in range(4)] + [("B", 32 + g, 4)]
            if full:
                chunks.append(("C", 40 + g, 5))
            pending = None
            for kind, j, c in chunks:
                ps, pb = proj_S([S_XBC + j], xnT, b_xnT)

                def src_fn(dst, bm, ps=ps, pb=pb):
                    CP("act", dst, ps[:, 0:T], [pb], [bm])
                acc, ba = conv(src_fn, 4, cols[:, C_MBW + 4 * j:C_MBW + 4 * j + 4], cols[:, C_MBB + j:C_MBB + j + 1],
                               xbch[:, j, :], b_xbch[j])
                if kind == "C":
                    ACT(CT_t[gb], acc[:], AF.Silu, [ba], [b_CT[gb]])
                else:
                    if kind == "B":
                        fT, bfT = BT_t[gb], b_BT[gb]
                    else:
                        k = rotate("fT", 2)
                        fT, bfT = fT_t[k], b_fT[k]
                    ACT(fT, acc[:], AF.Silu, [ba], [bfT])
                if pending is not None:
                    pending()
                    pending = None
                if kind != "C":
                    def pending(fT=fT, bfT=bfT, c=c):
                        pt, pbt = psum()
                        ptb = v3(pt[:, :].bitcast(BF16)[:, 0:T], 128)
                        for i in range(NTB):
                            TR(ptb[:, i, :], fT[:, i * 128:(i + 1) * 128], identb[:], [bfT, b_identb], [pbt])
                        CP("dve", xsB[:, :, c * 128:(c + 1) * 128], ptb, [pbt], [bx])
                yield
            if pending is not None:
                pending()
                pending = None
            if full:
                zps = [psum() for _ in range(NTB)]
                for kg in range(4):
                    slot, bs = wload("M", M_Z + g * 4 + kg)
                    for i in range(NTB):
                        for kcl in range(4):
                            kc = kg * 4 + kcl
                            MM(zps[i][0][:, :], xnT[:, kc, i * 128:(i + 1) * 128], slot[:, kcl * 512:(kcl + 1) * 512],
                               kc == 0, kc == KC - 1, [b_xnT, bs], [zps[i][1]])
                for i in range(NTB):
                    ACT(sz_t[gb][:, i, :], zps[i][0][:, :], AF.Silu, [zps[i][1]], [b_sz[gb]])
            yield

        def sc_phase():
            for i in range(16):
                psc, pbc = proj_S([S_SC + 3 * i], xnT, b_xnT)
                psh, pbh = proj_S([S_SC + 3 * i + 1], xnT, b_xnT)
                psb, pbb = proj_S([S_SC + 3 * i + 2], xnT, b_xnT)
                ft, bft = ftbuf()
                CP("act", ft[:], psc[:, 0:T], [pbc], [bft])

                def src_fn(dst, bm, ft=ft, bft=bft, psh=psh, pbh=pbh):
                    TT("dve", dst, ft[:], psh[:, 0:T], ALU.mult, [bft, pbh], [bm])
                acc, ba = conv(src_fn, 3, cols[:, C_SCW + 3 * i:C_SCW + 3 * i + 3], None, sch[:, i, :], b_sch[i])
                TT("dve", yaT[:, i, :], acc[:], psb[:, 0:T], ALU.mult, [ba, pbb], [b_yaT])

        def mix_phase():
            for i in range(16):
                psa, pba = proj_S([S_WA + i], yaT, b_yaT)
                psm, pbm = proj_S([S_WM + 2 * i, S_WM + 2 * i + 1], ymT, b_ymT)
                pga, pbga = proj_S([S_GATE + i], xnT, b_xnT)
                pgm, pbgm = proj_S([S_GATE + 16 + i], xnT, b_xnT)
                ga, bga = ftbuf()
                gm, bgm = ftbuf()
                ACT(ga[:], pga[:, 0:T], AF.Sigmoid, [pbga, b_cols], [bga], bias=cols[:, C_BG + i:C_BG + i + 1])
                ACT(gm[:], pgm[:, 0:T], AF.Sigmoid, [pbgm, b_cols], [bgm], bias=cols[:, C_BG + 16 + i:C_BG + 16 + i + 1])
                TT("dve", ga[:], ga[:], psa[:, 0:T], ALU.mult, [bga, pba], [bga])
                TT("dve", gm[:], gm[:], psm[:, 0:T], ALU.mult, [bgm, pbm], [bgm])
                TT("pool", mixT[:, i, :], ga[:], gm[:], ALU.add, [bga, bgm], [b_mixT])

        def tm_proj(unit0, nkg, nkc, lhsT_T, blhs, evac):
            for q in range(4):
                banks = [psum() for _ in range(NTB)]
                for kg in range(nkg):
                    slot, bs = wload("M", unit0 + q * nkg + kg)
                    for i in range(NTB):
                        for kcl in range(4):
                            kc = kg * 4 + kcl
                            if kc >= nkc:
                                continue
                            MM(banks[i][0][:, :], lhsT_T[:, kc, i * 128:(i + 1) * 128], slot[:, kcl * 512:(kcl + 1) * 512],
                               kc == 0, kc == nkc - 1, [blhs, bs], [banks[i][1]])
                for i in range(NTB):
                    evac(q, i, banks[i][0], banks[i][1])

        def ffn_up_phase():
            for i in range(NFC):
                psu, pbu = proj_S([S_UP + 2 * i], xnT, b_xnT)
                psv, pbv = proj_S([S_UP + 2 * i + 1], xnT, b_xnT)

                def src_fn(dst, bm, psu=psu, pbu=pbu):
                    CP("act", dst, psu[:, 0:T], [pbu], [bm])
                acc, ba = conv(src_fn, 3, cols[:, C_FFW + 3 * i:C_FFW + 3 * i + 3], cols[:, C_FFB + i:C_FFB + i + 1],
                               uh[:, i, :], b_uh[i])
                ft, bft = ftbuf()
                ACT(ft[:], acc[:], AF.Silu, [ba], [bft])
                TT("dve", actT[:, i, :], ft[:], psv[:, 0:T], ALU.mult, [bft, pbv], [b_actT])

        nblk_p = NPFX // NTB
        nblk_f = NF // NTB
        for blk in range(nblk_p + nblk_f):
            full = blk >= nblk_p
            t0 = blk * NTB
            if full:
                pc_tick(len(pc_pending))
            switch_alias(R2B, R2A)
            switch_alias(R1B, R1A)
            for i in range(NTB):
                tile = t0 + i
                k = tile % 2
                P.op("sp", lambda e, k=k, tile=tile: e.dma_start(out=xld[k][:], in_=xin[tile * 128:(tile + 1) * 128, :]),
                     (), [b_xld[k]], dsem=d_x[k])
                norm_transpose(xld[k][:], b_xld[k], xld[k][:], b_xld[k], C_G1, xnT, b_xnT, i)
            dt_phase(t0)
            for _ in group_phase(0, full):
                pass
            for g in range(NG):
                gB = group_B(g, full)
                gA = group_phase(g + 1, full) if g + 1 < NG else iter(())
                doneA = doneB = False
                while not (doneA and doneB):
                    if not doneB:
                        try:
                            next(gB)
                        except StopIteration:
                            doneB = True
                    if not doneA:
                        try:
                            next(gA)
                        except StopIteration:
                            doneA = True
            if not full:
                pc_tick(-(-320 // max(1, nblk_p - 1)))
                continue
            sc_phase()
            switch_alias(R2A, R2B)
            mix_phase()
            for i in range(NTB):
                tile = t0 + i
                P.op("pool", lambda e, i=i, tile=tile: e.dma_start(out=h1[:, i, :], in_=xin[tile * 128:(tile + 1) * 128, :]),
                     (), [b_h1[i]], dsem=d_h[i])

            def evac_h(q, i, ps, pb):
                TT("dve", h1[:, i, q * 512:(q + 1) * 512], ps[:, :], h1[:, i, q * 512:(q + 1) * 512], ALU.add,
                   [pb, b_h1[i]], [b_h1[i]])
            tm_proj(M_WO, 4, KC, mixT, b_mixT, evac_h)
            for i in range(NTB):
                k = i % 2
                norm_transpose(h1[:, i, :], b_h1[i], xld[k][:], b_xld[k], C_G2, xnT, b_xnT, i)
            switch_alias(R1A, R1B)
            ffn_up_phase()
            tm_proj(M_DN, 11, NFC, actT, b_actT, evac_h)
            for i in range(NTB):
                ftile = (blk - nblk_p) * NTB + i
                if ftile == 0 or ftile > NT:
                    continue
                k = i % 2
                ssq, bssq = sm()
                ACT(junk[:], h1[:, i, :], AF.Square, [b_h1[i]], [bssq], scale=float(D ** -0.5), accum=ssq)
                rstd, brstd = rstd_from_ssq(ssq, bssq)
                STT("dve", xld[k][:], h1[:, i, :], rstd, gfbc[:], ALU.mult, ALU.mult, [b_h1[i], brstd, b_gf], [b_xld[k]])
                orow = (ftile - 1) * 128
                P.op("pool", lambda e, k=k, orow=orow: e.dma_start(out=outd[orow:orow + 128, :], in_=xld[k][:]),
                     [b_xld[k]], (), dsem=d_o[k])
        P.run(final_dsems=d_o)
        nc._sem_counts = dict(P.cnt)
    return nc


def _s_unit(W, col0, k0=0):
    blk = W[k0:k0 + 2048, col0:col0 + 128]
    return blk.reshape(16, 128, 128).transpose(1, 0, 2).reshape(128, 2048)


def _m_unit(W, col0, k0):
    K = W.shape[0]
    blk = np.zeros((512, 512), np.float32)
    n = max(0, min(512, K - k0))
    blk[:n] = W[k0:k0 + n, col0:col0 + 512]
    return blk.reshape(4, 128, 512).transpose(1, 0, 2).reshape(128, 2048)


def _colp(v):
    return np.ascontiguousarray(v.reshape(-1, 128).T)


def _convp(w):
    K, C = w.shape
    return np.ascontiguousarray(w.T.reshape(C // 128, 128, K).transpose(1, 0, 2).reshape(128, -1))


def prep_weights(norm1_g, w_in, b_gate, sc_conv_w, mb_conv_w, mb_conv_b, dt_bias, a_log, d_skip, mb_norm_g,
                 w_a, w_m, w_o, norm2_g, w_up, ffn_conv_w, ffn_conv_b, w_down, normf_g):
    f = lambda a: np.asarray(a, np.float32)
    w_in, w_a, w_m, w_o, w_up, w_down = f(w_in)[0], f(w_a)[0], f(w_m)[0], f(w_o)[0], f(w_up)[0], f(w_down)[0]
    WS = np.empty((NSU, 128, 2048), np.float32)
    WM = np.empty((NMU, 128, 2048), np.float32)
    o_scb, o_scc, o_sch, o_z, o_xbc, o_dt, o_gate = 0, 2048, 4096, 6144, 10240, 16384, 16448
    for i in range(16):
        WS[S_SC + 3 * i] = _s_unit(w_in, o_scc + i * 128)
        WS[S_SC + 3 * i + 1] = _s_unit(w_in, o_sch + i * 128)
        WS[S_SC + 3 * i + 2] = _s_unit(w_in, o_scb + i * 128)
    for j in range(48):
        WS[S_XBC + j] = _s_unit(w_in, o_xbc + j * 128)
    for j in range(32):
        WS[S_GATE + j] = _s_unit(w_in, o_gate + j * 128)
    for i in range(16):
        WS[S_WA + i] = _s_unit(w_a, i * 128)
        WS[S_WM + 2 * i] = _s_unit(w_m, i * 128, 0)
        WS[S_WM + 2 * i + 1] = _s_unit(w_m, i * 128, 2048)
    for i in range(NFC):
        WS[S_UP + 2 * i] = _s_unit(w_up, i * 128)
        WS[S_UP + 2 * i + 1] = _s_unit(w_up, DFF + i * 128)
    for g in range(8):
        for kg in range(4):
            WM[M_Z + g * 4 + kg] = _m_unit(w_in, o_z + g * 512, kg * 512)
    for q in range(4):
        for kg in range(4):
            WM[M_WO + q * 4 + kg] = _m_unit(w_o, q * 512, kg * 512)
        for kg in range(11):
            WM[M_DN + q * 11 + kg] = _m_unit(w_down, q * 512, kg * 512)
    WDT = np.ascontiguousarray(w_in[:, o_dt:o_dt + 64].reshape(16, 128, 64).transpose(1, 0, 2).reshape(128, 1024))
    cols = np.zeros((128, NCOLS), np.float32)
    cols[:, C_G1:C_G1 + 16] = _colp(f(norm1_g)[0])
    cols[:, C_G2:C_G2 + 16] = _colp(f(norm2_g)[0])
    cols[:, C_SCW:C_SCW + 48] = _convp(f(sc_conv_w)[0])
    cols[:, C_MBW:C_MBW + 192] = _convp(f(mb_conv_w)[0])
    cols[:, C_MBB:C_MBB + 48] = _colp(f(mb_conv_b)[0])
    cols[:, C_FFW:C_FFW + 129] = _convp(f(ffn_conv_w)[0])
    cols[:, C_FFB:C_FFB + 43] = _colp(f(ffn_conv_b)[0])
    cols[:, C_BG:C_BG + 32] = _colp(f(b_gate)[0])
    cols[:, C_MBG:C_MBG + 32] = _colp(f(mb_norm_g)[0])
    rows = np.concatenate([f(dt_bias)[0], f(a_log)[0], f(d_skip)[0]]).astype(np.float32)
    return {"WS": WS, "WM": WM, "WDT": WDT, "cols": cols, "rows": rows, "gf": np.ascontiguousarray(f(normf_g))}


def prep_streams(x, meta_tokens, NT, CPB, NTB):
    x = np.asarray(x, np.float32)
    meta = np.asarray(meta_tokens, np.float32)
    B = x.shape[0]
    NPFX, NF = _rup((CPB - 1) * NT, NTB), _rup(NT + 1, NTB)
    NTILES = NPFX + NF
    NEND = NPFX + NT + 1
    streams, masks = [], []
    for b in range(B):
        seq = np.zeros(((CPB * NT + 1) * 128, D), np.float32)
        seq[128 - N_META:128] = meta
        seq[128:] = x[b]
        valid = np.ones(((CPB * NT + 1) * 128,), np.float32)
        valid[:128 - N_META] = 0.0
        for j in range(CPB):
            n_real = (j + 1) * NT + 1
            s = np.zeros((NTILES * 128, D), np.float32)
            m = np.zeros((NTILES * 128,), np.float32)
            s[(NEND - n_real) * 128:NEND * 128] = seq[:n_real * 128]
            m[(NEND - n_real) * 128:NEND * 128] = valid[:n_real * 128]
            streams.append(s)
            masks.append(np.ascontiguousarray(m.reshape(NTILES, 128).T))
    return streams, masks


_CACHE = {}


def run(inputs, NT, NTB, CPB):
    key = (NT, NTB, CPB)
    if key not in _CACHE:
        _CACHE[key] = build(NT, NTB, CPB)
    nc = _CACHE[key]
    wd = prep_weights(**{k: v for k, v in inputs.items() if k not in ("x", "meta_tokens")})
    streams, masks = prep_streams(inputs["x"], inputs["meta_tokens"], NT, CPB, NTB)
    B = 2
    ncores = B * CPB
    in_maps = []
    for c in range(ncores):
        m = dict(wd)
        m["xin"] = streams[c]
        m["mask"] = masks[c]
        in_maps.append(m)
    res = run_bass_kernel_spmd(nc, in_maps, core_ids=list(range(ncores)))
    out = np.empty((B, CPB * NT * 128, D), np.float32)
    for c in range(ncores):
        b, j = divmod(c, CPB)
        out[b, j * NT * 128:(j + 1) * NT * 128] = res.results[c]["out"]
    return out


CPB_FULL = 4


def kernel(**inputs):
    return run(inputs, SEQ // 128 // CPB_FULL, 3, CPB_FULL)
```

```python
import numpy as np
from contextlib import ExitStack
import concourse.bass as bass
import concourse.mybir as mybir
from concourse.bass_utils import run_bass_kernel_spmd

F32 = mybir.dt.float32
BF16 = mybir.dt.bfloat16
AF = mybir.ActivationFunctionType
ALU = mybir.AluOpType

D = 2048
KC = 16
DI = 4096
NG = 8
NH = 64
DFF = 5504
NFC = 43
EPS = 1e-6
N_META = 16
SEQ = 16384
BATCH = 2

S_SC, S_XBC, S_GATE, S_WA, S_WM, S_UP, NSU = 0, 48, 96, 128, 144, 176, 262
M_Z, M_WO, M_DN, NMU = 0, 32, 48, 92
C_G1, C_G2, C_SCW, C_MBW, C_MBB, C_FFW, C_FFB, C_BG, C_MBG, NCOLS = 0, 16, 32, 80, 272, 320, 449, 492, 524, 556

ENGS = ("pe", "act", "dve", "pool", "sp")
INORDER = ("pe", "act", "dve", "pool")


class Buf:
    __slots__ = ("name", "w", "r")

    def __init__(self, name):
        self.name = name
        self.w = None
        self.r = {}


class DSem:
    __slots__ = ("h", "count")

    def __init__(self, h):
        self.h = h
        self.count = 0


class Rec:
    __slots__ = ("eng", "fn", "deps", "need", "sem", "val", "dma", "idx")


class Prog:
    def __init__(self, nc, stack):
        self.nc = nc
        self.ops = {e: [] for e in ENGS}
        self.esem = {}
        for e in INORDER:
            self.esem[e] = stack.enter_context(nc.semaphore("s_" + e))
        self.stack = stack
        self.nrec = 0

    def dsem(self, name):
        return DSem(self.stack.enter_context(self.nc.semaphore(name)))

    def op(self, eng, fn, reads=(), writes=(), dsem=None):
        rec = Rec()
        rec.eng = eng
        rec.fn = fn
        rec.need = dsem is not None
        rec.dma = dsem
        rec.sem = None
        rec.val = 0
        rec.idx = self.nrec
        self.nrec += 1
        deps = {}
        wr = {}
        for b in reads:
            if b.w is not None:
                deps[id(b.w)] = b.w
                wr[id(b.w)] = True
        for b in writes:
            if b.w is not None:
                deps[id(b.w)] = b.w
                wr[id(b.w)] = True
            for r in b.r.values():
                deps[id(r)] = r
        out = []
        isdma = dsem is not None
        for k, d in deps.items():
            if d is rec:
                continue
            if d.dma is None and not isdma and d.eng == eng:
                if eng == "pe":
                    continue
                if k not in wr:
                    continue
            d.need = True
            out.append(d)
        rec.deps = out
        key = ("dma", rec.idx) if isdma else eng
        for b in reads:
            b.r[key] = rec
        for b in writes:
            b.w = rec
            b.r = {}
        self.ops[eng].append(rec)
        return rec

    def finalize(self):
        cnt = {e: 0 for e in INORDER}
        for e in ENGS:
            for rec in self.ops[e]:
                if rec.dma is not None:
                    rec.dma.count += 16
                    rec.sem = rec.dma.h
                    rec.val = rec.dma.count
                elif rec.need:
                    cnt[e] += 1
                    rec.sem = self.esem[e]
                    rec.val = cnt[e]
        self.cnt = cnt

    def emit_engine(self, e, eng, final_dsems=()):
        waited = {}
        for rec in self.ops[e]:
            for d in rec.deps:
                k = id(d.sem)
                if waited.get(k, 0) >= d.val:
                    continue
                waited[k] = d.val
                eng.wait_ge(d.sem, d.val)
            ins = rec.fn(eng)
            if rec.dma is not None:
                ins.then_inc(rec.sem, 16)
            elif rec.need:
                ins.then_inc(rec.sem, 1)
        for ds in final_dsems:
            if ds.count:
                eng.wait_ge(ds.h, ds.count)

    def run(self, final_dsems=()):
        self.finalize()
        nc = self.nc
        with nc.Block() as block:
            @block.tensor
            def _(e):
                self.emit_engine("pe", e)

            @block.scalar
            def _(e):
                self.emit_engine("act", e)

            @block.vector
            def _(e):
                self.emit_engine("dve", e)

            @block.gpsimd
            def _(e):
                self.emit_engine("pool", e, final_dsems)

            @block.sync
            def _(e):
                self.emit_engine("sp", e)


def switch_alias(old_bufs, new_bufs):
    accs = {}
    for b in old_bufs:
        if b.w is not None:
            accs[id(b.w)] = b.w
        for r in b.r.values():
            accs[id(r)] = r
    for n in new_bufs:
        n.w = None
        n.r = {("al", k): rec for k, rec in accs.items()}


def _rup(a, b):
    return -(-a // b) * b


def build(NT, NTB, CPB=4, NSLOT=8):
    NF = _rup(NT + 1, NTB)
    NPFX = _rup((CPB - 1) * NT, NTB)
    NTILES = NPFX + NF
    assert NF % NTB == 0 and NPFX % NTB == 0
    T = NTB * 128
    nc = bass.Bass("TRN2", target_bir_lowering=False)
    xin = nc.dram_tensor("xin", [NTILES * 128, D], F32, kind="ExternalInput").ap()
    maskd = nc.dram_tensor("mask", [128, NTILES], F32, kind="ExternalInput").ap()
    WS = nc.dram_tensor("WS", [NSU, 128, 2048], F32, kind="ExternalInput").ap()
    WM = nc.dram_tensor("WM", [NMU, 128, 2048], F32, kind="ExternalInput").ap()
    WDT = nc.dram_tensor("WDT", [128, 1024], F32, kind="ExternalInput").ap()
    colsd = nc.dram_tensor("cols", [128, NCOLS], F32, kind="ExternalInput").ap()
    rowsd = nc.dram_tensor("rows", [192], F32, kind="ExternalInput").ap()
    gfd = nc.dram_tensor("gf", [D], F32, kind="ExternalInput").ap()
    outd = nc.dram_tensor("out", [NT * 128, D], F32, kind="ExternalOutput").ap()
    WSb = nc.dram_tensor("WSb", [NSU, 128, 2048], BF16, kind="Internal").ap()
    WMb = nc.dram_tensor("WMb", [NMU, 128, 2048], BF16, kind="Internal").ap()
    WDTb = nc.dram_tensor("WDTb", [128, 1024], BF16, kind="Internal").ap()

    with ExitStack() as st:
        P = Prog(nc, st)

        def sb(name, shape, dt):
            return st.enter_context(nc.sbuf_tensor(name, shape, dt))

        def MM(out, lhsT, rhs, start, stop, r, w):
            P.op("pe", lambda e: e.matmul(out, lhsT=lhsT, rhs=rhs, start=start, stop=stop), r, w)

        def TR(out, in_, ident, r, w):
            P.op("pe", lambda e: e.transpose(out=out, in_=in_, identity=ident), r, w)

        def ACT(out, in_, func, r, w, bias=None, scale=None, accum=None):
            kw = {}
            if bias is not None:
                kw["bias"] = bias
            if scale is not None:
                kw["scale"] = scale
            if accum is not None:
                kw["accum_out"] = accum
            P.op("act", lambda e: e.activation(out=out, in_=in_, func=func, **kw), r, w)

        def TT(eng, out, in0, in1, op, r, w):
            P.op(eng, lambda e: e.tensor_tensor(out=out, in0=in0, in1=in1, op=op), r, w)

        def TS(eng, out, in0, s1, s2, op0, op1, r, w):
            P.op(eng, lambda e: e.tensor_scalar(out=out, in0=in0, scalar1=s1, scalar2=s2, op0=op0, op1=op1), r, w)

        def STT(eng, out, in0, scalar, in1, op0, op1, r, w):
            P.op(eng, lambda e: e.scalar_tensor_tensor(out=out, in0=in0, scalar=scalar, in1=in1, op0=op0, op1=op1), r, w)

        def CP(eng, out, in_, r, w):
            if eng == "act":
                P.op("act", lambda e: e.activation(out=out, in_=in_, func=AF.Copy), r, w)
            else:
                P.op(eng, lambda e: e.tensor_copy(out=out, in_=in_), r, w)

        def MSET(eng, ap, val, w):
            P.op(eng, lambda e: e.memset(ap, val), (), w)

        def bc3(ap2, n, pos):
            m = ap2.shape[1]
            if pos == 2:
                return ap2.unsqueeze(2).to_broadcast([128, m, n])
            return ap2.unsqueeze(1).to_broadcast([128, n, m])

        def v3(ap2, b):
            return ap2.rearrange("p (a b) -> p a b", b=b)

        identf = sb("identf", [128, 128], F32); b_identf = Buf("identf")
        identb = sb("identb", [128, 128], BF16); b_identb = Buf("identb")
        triLE = sb("triLE", [128, 128], F32); b_tri = Buf("tri")
        U2 = sb("U2", [128, 128], F32); b_U2 = Buf("U2")
        onesf = sb("onesf", [128, 128], F32); b_ones = Buf("ones")
        cols = sb("cols_sb", [128, NCOLS], F32); b_cols = Buf("cols")
        rows = sb("rows_sb", [128, 192], F32); b_rows = Buf("rows")
        Abc = sb("Abc", [128, 64], F32); b_A = Buf("A")
        gfbc = sb("gfbc", [128, D], F32); b_gf = Buf("gf")
        maskt = sb("maskt", [128, NTILES], F32); b_mask = Buf("mask")
        ccol = sb("ccol", [128, 3], F32); b_ccol = Buf("ccol")
        junk = sb("junk", [128, D], BF16)
        dtb_bc = rows[:, 0:64]
        D_bc = rows[:, 128:192]

        d_c = [P.dsem("d_c%d" % k) for k in range(4)]
        P.op("sp", lambda e: e.dma_start(out=cols[:], in_=colsd[:, :]), (), [b_cols], dsem=d_c[0])
        P.op("sp", lambda e: e.dma_start(out=rows[:], in_=rowsd.partition_broadcast(128)), (), [b_rows], dsem=d_c[1])
        P.op("sp", lambda e: e.dma_start(out=gfbc[:], in_=gfd.partition_broadcast(128)), (), [b_gf], dsem=d_c[2])
        P.op("sp", lambda e: e.dma_start(out=maskt[:], in_=maskd[:, :]), (), [b_mask], dsem=d_c[3])

        MSET("pool", identf[:], 1.0, [b_identf])
        P.op("pool", lambda e: e.affine_select(out=identf[:], in_=identf[:], pattern=[[-1, 128]], compare_op=ALU.is_equal,
                                               fill=0.0, base=0, channel_multiplier=1), [b_identf], [b_identf])
        CP("dve", identb[:], identf[:], [b_identf], [b_identb])
        MSET("pool", triLE[:], 1.0, [b_tri])
        P.op("pool", lambda e: e.affine_select(out=triLE[:], in_=triLE[:], pattern=[[1, 128]], compare_op=ALU.is_ge,
                                               fill=0.0, base=0, channel_multiplier=-1), [b_tri], [b_tri])
        MSET("pool", U2[:], 1.0, [b_U2])
        P.op("pool", lambda e: e.affine_select(out=U2[:], in_=U2[:], pattern=[[-1, 128]], compare_op=ALU.is_gt,
                                               fill=0.0, base=0, channel_multiplier=1), [b_U2], [b_U2])
        MSET("dve", onesf[:], 1.0, [b_ones])
        MSET("dve", ccol[:, 0:1], 1.0, [b_ccol])
        MSET("dve", ccol[:, 1:2], EPS, [b_ccol])
        MSET("dve", ccol[:, 2:3], 0.0, [b_ccol])
        one_c = ccol[:, 0:1]
        eps_c = ccol[:, 1:2]
        zero_c = ccol[:, 2:3]
        ACT(Abc[:], rows[:, 64:128], AF.Exp, [b_rows], [b_A])
        TS("dve", Abc[:], Abc[:], -1.0, 0.0, ALU.mult, ALU.add, [b_A], [b_A])

        S = sb("S", [128, NG, 512], F32); b_S = [Buf("S%d" % g) for g in range(NG)]
        xbch = sb("xbch", [128, 48, 3], F32); b_xbch = [Buf("xbch%d" % j) for j in range(48)]
        sch = sb("sch", [128, 16, 2], F32); b_sch = [Buf("sch%d" % j) for j in range(16)]
        uh = sb("uh", [128, NFC, 2], F32); b_uh = [Buf("uh%d" % j) for j in range(NFC)]
        MSET("pool", S[:], 0.0, b_S)
        MSET("pool", xbch[:], 0.0, b_xbch)
        MSET("pool", sch[:], 0.0, b_sch)
        MSET("pool", uh[:], 0.0, b_uh)

        NSM = 64
        smt = sb("smt", [128, NSM], F32); b_sm = [Buf("sm%d" % k) for k in range(NSM)]
        sm_pos = [0]

        def sm():
            k = sm_pos[0] % NSM
            sm_pos[0] += 1
            return smt[:, k:k + 1], b_sm[k]

        dtt = sb("dtt", [128, NTB, 5, 64], F32)
        b_dt = [[Buf("dt%d_%d" % (i, k)) for k in range(5)] for i in range(NTB)]
        dtmp = sb("dtmp", [128, 4, 192], F32); b_dtmp = [Buf("dtmp%d" % k) for k in range(4)]

        xnT = sb("xnT", [128, KC, T], BF16); b_xnT = Buf("xnT")
        xld = [sb("xld%d" % k, [128, D], F32) for k in range(2)]; b_xld = [Buf("xld0"), Buf("xld1")]
        d_x = [P.dsem("d_x0"), P.dsem("d_x1")]
        d_o = [P.dsem("d_o0"), P.dsem("d_o1")]
        d_h = [P.dsem("d_h%d" % i) for i in range(NTB)]

        NRAW = 3
        rawt = [sb("raw%d" % k, [128, T + 3], F32) for k in range(NRAW)]
        b_rawh = [Buf("rawh%d" % k) for k in range(NRAW)]
        b_rawm = [Buf("rawm%d" % k) for k in range(NRAW)]
        acct = [sb("acc%d" % k, [128, T], F32) for k in range(NRAW)]
        b_acc = [Buf("acc%d" % k) for k in range(NRAW)]
        raw_pos = [0]
        ftmp = [sb("ftmp%d" % k, [128, T], F32) for k in range(2)]; b_ftmp = [Buf("ftmp0"), Buf("ftmp1")]
        ft_pos = [0]

        def ftbuf():
            k = ft_pos[0] % 2
            ft_pos[0] += 1
            return ftmp[k], b_ftmp[k]

        R1 = sb("R1", [128, 48 * T], BF16)
        ymT = v3(R1[:, 0:32 * T], T); b_ymT = Buf("ymT")
        yaT = v3(R1[:, 32 * T:48 * T], T); b_yaT = Buf("yaT")
        actT = v3(R1[:, 0:44 * T], T); b_actT = Buf("actT")

        off = [0]
        r2items = []

        def r2(nel_bf16):
            o = off[0]
            off[0] += nel_bf16
            return o

        o_xsB = [r2(NTB * 640) for _ in range(2)]
        o_BT = [r2(T) for _ in range(2)]
        o_CT = [r2(T) for _ in range(2)]
        o_sz = [r2(NTB * 512) for _ in range(2)]
        o_fT = [r2(T) for _ in range(2)]
        o_rs = [r2(2048) for _ in range(2)]
        o_E = [r2(1024) for _ in range(2)]
        o_MT = [r2(1024) for _ in range(2)]
        o_cbm = [r2(128) for _ in range(2)]
        o_xdt = [r2(512) for _ in range(2)]
        o_si = [r2(512) for _ in range(2)]
        o_Sb = [r2(512) for _ in range(2)]
        o_y = [r2(1024) for _ in range(4)]
        szA = off[0]
        szB = 16 * T + NTB * D * 2
        R2 = sb("R2", [128, max(szA, szB)], BF16)

        def r2v(o, n, dt=BF16):
            ap = R2[:, o:o + n]
            return ap.bitcast(F32) if dt == F32 else ap

        xsB_t = [v3(r2v(o, NTB * 640), 640) for o in o_xsB]; b_xsB = [Buf("xsB0"), Buf("xsB1")]
        BT_t = [r2v(o, T) for o in o_BT]; b_BT = [Buf("BT0"), Buf("BT1")]
        CT_t = [r2v(o, T) for o in o_CT]; b_CT = [Buf("CT0"), Buf("CT1")]
        sz_t = [v3(r2v(o, NTB * 512), 512) for o in o_sz]; b_sz = [Buf("sz0"), Buf("sz1")]
        fT_t = [r2v(o, T) for o in o_fT]; b_fT = [Buf("fT0"), Buf("fT1")]
        rs_t = [v3(r2v(o, 2048, F32), 128) for o in o_rs]; b_rs = [Buf("rs0"), Buf("rs1")]
        E_t = [v3(r2v(o, 1024), 128) for o in o_E]; b_E = [Buf("E0"), Buf("E1")]
        MT_t = [v3(r2v(o, 1024), 128) for o in o_MT]; b_MT = [Buf("MT0"), Buf("MT1")]
        cbm_t = [r2v(o, 128) for o in o_cbm]; b_cbm = [Buf("cbm0"), Buf("cbm1")]
        xdt_t = [r2v(o, 512) for o in o_xdt]; b_xdt = [Buf("xdt0"), Buf("xdt1")]
        si_t = [r2v(o, 512) for o in o_si]; b_si = [Buf("si0"), Buf("si1")]
        Sb_t = [r2v(o, 512) for o in o_Sb]; b_Sb = [Buf("Sb0"), Buf("Sb1")]
        y_t = [r2v(o, 1024, F32) for o in o_y]; b_y = [Buf("y%d" % k) for k in range(4)]
        R2A = b_xsB + b_BT + b_CT + b_sz + b_fT + b_rs + b_E + b_MT + b_cbm + b_xdt + b_si + b_Sb + b_y
        mixT = v3(R2[:, 0:16 * T], T); b_mixT = Buf("mixT")
        h1 = v3(R2[:, 16 * T:16 * T + NTB * D * 2].bitcast(F32), D); b_h1 = [Buf("h1_%d" % i) for i in range(NTB)]
        R2B = [b_mixT] + b_h1
        R1A = [b_ymT, b_yaT]
        R1B = [b_actT]
        rot = {}

        def rotate(name, n):
            k = rot.get(name, 0)
            rot[name] = k + 1
            return k % n

        pst = [st.enter_context(nc.psum_tensor("ps%d" % k, [128, 512], F32)) for k in range(8)]
        b_ps = [Buf("ps%d" % k) for k in range(8)]
        ps_pos = [0]

        def psum():
            k = ps_pos[0] % 8
            ps_pos[0] += 1
            return pst[k], b_ps[k]

        ring = [sb("ring%d" % s, [128, 2048], BF16) for s in range(NSLOT)]
        b_ring = [Buf("ring%d" % s) for s in range(NSLOT)]
        d_ring = [P.dsem("d_ring%d" % s) for s in range(NSLOT)]
        ring_pos = [0]
        NPC = 40
        d_pc = [P.dsem("d_pc%d" % k) for k in range(NPC)]
        b_pcs = [Buf("pcs%d" % k) for k in range(NPC)]
        b_wd = {}
        pc_pos = [0]

        pc_pending = []

        def pc_tick(n):
            for _ in range(n):
                if pc_pending:
                    precast_now(*pc_pending.pop(0))

        def precast(kind, idx, now=False):
            if now:
                precast_now(kind, idx)
            else:
                pc_pending.append((kind, idx))

        def precast_now(kind, idx):
            key = (kind, idx)
            if key in b_wd:
                return
            b = Buf("wd_%s%d" % (kind, idx))
            b_wd[key] = b
            k = pc_pos[0] % NPC
            pc_pos[0] += 1
            if kind == "S":
                src, dst = WS[idx], WSb[idx]
            elif kind == "M":
                src, dst = WM[idx], WMb[idx]
            else:
                src, dst = WDT[:, :], WDTb[:, :]
            P.op("pool", lambda e: e.dma_start(out=dst, in_=src), (), [b, b_pcs[k]], dsem=d_pc[k])

        def wload(kind, idx):
            s = ring_pos[0] % NSLOT
            ring_pos[0] += 1
            if kind == "S":
                src, n = WSb[idx], 2048
            elif kind == "M":
                src, n = WMb[idx], 2048
            else:
                src, n = WDTb[:, :], 1024
            dst = ring[s][:, 0:n]
            P.op("sp", lambda e: e.dma_start(out=dst, in_=src), [b_wd[(kind, idx)]], [b_ring[s]], dsem=d_ring[s])
            return ring[s], b_ring[s]

        precast("D", 0, True)
        for g in range(NG):
            for c in range(4):
                precast("S", S_XBC + g * 4 + c, True)
            precast("S", S_XBC + 32 + g, True)
        for g in range(NG):
            precast("S", S_XBC + 40 + g)
            for kg in range(4):
                precast("M", M_Z + g * 4 + kg)
        for i in range(48):
            precast("S", S_SC + i)
        for i in range(16):
            precast("S", S_WA + i)
            precast("S", S_WM + 2 * i)
            precast("S", S_WM + 2 * i + 1)
            precast("S", S_GATE + i)
            precast("S", S_GATE + 16 + i)
        for i in range(16):
            precast("M", M_WO + i)
        for i in range(86):
            precast("S", S_UP + i)
        for i in range(44):
            precast("M", M_DN + i)

        def rstd_from_ssq(ssq, bssq):
            lnv, bl = sm()
            ACT(lnv, ssq, AF.Ln, [bssq, b_ccol], [bl], bias=eps_c)
            rs_, br = sm()
            ACT(rs_, lnv, AF.Exp, [bl], [br], scale=-0.5)
            return rs_, br

        def norm_transpose(src, bsrc, dst_tm, bdst, gcol_off, dstT, bdstT, i):
            ssq, bssq = sm()
            ACT(junk[:], src, AF.Square, [bsrc], [bssq], scale=float(D ** -0.5), accum=ssq)
            rs_, br = rstd_from_ssq(ssq, bssq)
            ACT(dst_tm, src, AF.Copy, [bsrc, br], [bdst], scale=rs_)
            for q in range(4):
                ps, pb = psum()
                for kk in range(4):
                    c = 4 * q + kk
                    TR(ps[:, kk * 128:(kk + 1) * 128], dst_tm[:, c * 128:(c + 1) * 128], identf[:], [bdst, b_identf], [pb])
                TT("dve", dstT[:, 4 * q:4 * q + 4, i * 128:(i + 1) * 128], v3(ps[:, :], 128),
                   bc3(cols[:, gcol_off + 4 * q:gcol_off + 4 * q + 4], 128, 2), ALU.mult, [pb, b_cols], [bdstT])

        def proj_S(units, rhsT, brhs):
            ps, pb = psum()
            nk = len(units) * KC
            k = 0
            for ui, u in enumerate(units):
                slot, bs = wload("S", u)
                for kc in range(KC):
                    MM(ps[:, 0:T], slot[:, kc * 128:(kc + 1) * 128], rhsT[:, ui * KC + kc, :], k == 0, k == nk - 1,
                       [bs, brhs], [pb])
                    k += 1
            return ps, pb

        def conv(src_fn, K, wcol, bias, halo, bhalo):
            k = raw_pos[0] % NRAW
            raw_pos[0] += 1
            raw, bh, bm = rawt[k], b_rawh[k], b_rawm[k]
            acc, ba = acct[k], b_acc[k]
            CP("dve", raw[:, 0:K - 1], halo, [bhalo], [bh])
            src_fn(raw[:, K - 1:K - 1 + T], bm)
            CP("dve", halo, raw[:, T:T + K - 1], [bm], [bhalo])
            TS("dve", acc[:], raw[:, K - 1:K - 1 + T], wcol[:, K - 1:K], bias if bias is not None else zero_c,
               ALU.mult, ALU.add, [bm, b_cols, b_ccol], [ba])
            for kk in range(K - 1):
                STT("dve", acc[:], raw[:, kk:kk + T], wcol[:, kk:kk + 1], acc[:], ALU.mult, ALU.add,
                    [bm, bh, b_cols, ba], [ba])
            return acc, ba

        def dt_phase(t0):
            slot, bs = wload("D", 0)
            for i in range(NTB):
                tile = t0 + i
                ps, pb = psum()
                for kc in range(KC):
                    MM(ps[:, 0:64], xnT[:, kc, i * 128:(i + 1) * 128], slot[:, kc * 64:(kc + 1) * 64], kc == 0, kc == KC - 1,
                       [b_xnT, bs], [pb])
                k = rotate("dtmp", 4)
                tm, btm = dtmp[:, k, :], b_dtmp[k]
                xr, ax, ex, ln_ = tm[:, 0:64], tm[:, 64:128], tm[:, 128:192], tm[:, 64:128]
                dt_i, dtA_i, od_i, w_i, cd_i = [dtt[:, i, q, :] for q in range(5)]
                bdt, bdtA, bod, bw, bcd = b_dt[i]
                TT("dve", xr, ps[:, 0:64], dtb_bc, ALU.add, [pb, b_rows], [btm])
                ACT(ax, xr, AF.Abs, [btm], [btm])
                ACT(ex, ax, AF.Exp, [btm], [btm], scale=-1.0)
                ACT(ln_, ex, AF.Ln, [btm, b_ccol], [btm], bias=one_c)
                TS("dve", xr, xr, 0.0, 0.0, ALU.max, ALU.add, [btm], [btm])
                TT("dve", xr, xr, ln_, ALU.add, [btm], [btm])
                TS("dve", dt_i, xr, maskt[:, tile:tile + 1], zero_c, ALU.mult, ALU.add, [btm, b_mask, b_ccol], [bdt])
                TT("dve", dtA_i, dt_i, Abc[:], ALU.mult, [bdt, b_A], [bdtA])
                ps3, pb3 = psum()
                MM(ps3[:, 0:64], triLE[:], dtA_i, True, True, [b_tri, bdtA], [pb3])
                MM(ps3[:, 64:128], U2[:], dtA_i, True, True, [b_U2, bdtA], [pb3])
                MM(ps3[:, 128:192], onesf[:], dtA_i, True, True, [b_ones, bdtA], [pb3])
                ACT(od_i, ps3[:, 0:64], AF.Exp, [pb3], [bod])
                ACT(ex, ps3[:, 64:128], AF.Exp, [pb3], [btm])
                ACT(cd_i, ps3[:, 128:192], AF.Exp, [pb3], [bcd])
                TT("dve", w_i, ex, dt_i, ALU.mult, [btm, bdt], [bw])

        def ssd_tile(g, i, full, gb):
            hs0 = g * 8
            xsB, bx = xsB_t[gb], b_xsB[gb]
            xs_t = xsB[:, i, 0:512]
            xs3 = v3(xs_t, 64)
            B_t = xsB[:, i, 512:640]
            dt_i, dtA_i, od_i, w_i, cd_i = [dtt[:, i, q, hs0:hs0 + 8] for q in range(5)]
            bdt, bdtA, bod, bw, bcd = b_dt[i]
            tok = slice(i * 128, (i + 1) * 128)
            Sg = S[:, g, :]
            if full:
                k = rotate("rs", 2)
                rs_, brs = rs_t[k], b_rs[k]
                TT("pool", rs_, bc3(dtA_i, 128, 2), bc3(triLE[:, :], 8, 1), ALU.mult, [bdtA, b_tri], [brs])
                psA, pbA = psum()
                psB, pbB = psum()
                MM(psA[:, :], U2[:], rs_[:, 0:4, :], True, True, [b_U2, brs], [pbA])
                MM(psB[:, :], U2[:], rs_[:, 4:8, :], True, True, [b_U2, brs], [pbB])
                k = rotate("E", 2)
                E, bE = E_t[k], b_E[k]
                ACT(E[:, 0:4, :], v3(psA[:, :], 128), AF.Exp, [pbA], [bE])
                ACT(E[:, 4:8, :], v3(psB[:, :], 128), AF.Exp, [pbB], [bE])
                psc, pbc = psum()
                MM(psc[:, 0:128], BT_t[gb][:, tok], CT_t[gb][:, tok], True, True, [b_BT[gb], b_CT[gb]], [pbc])
                k = rotate("cbm", 2)
                cbm, bcbm = cbm_t[k], b_cbm[k]
                TT("dve", cbm, psc[:, 0:128], triLE[:], ALU.mult, [pbc, b_tri], [bcbm])
                k = rotate("MT", 2)
                MT, bMT = MT_t[k], b_MT[k]
                TT("dve", MT, E, bc3(cbm, 8, 1), ALU.mult, [bE, bcbm], [bMT])
                k = rotate("xdt", 2)
                xdt, bxdt = xdt_t[k], b_xdt[k]
                TT("pool", v3(xdt, 64), xs3, bc3(dt_i, 64, 2), ALU.mult, [bx, bdt], [bxdt])
                k = rotate("Sb", 2)
                Sb, bSb = Sb_t[k], b_Sb[k]
                CP("pool", Sb, Sg, [b_S[g]], [bSb])
                k = rotate("y", 4)
                y2, by2 = y_t[k], b_y[k]
                TT("pool", v3(y2, 64), xs3, bc3(D_bc[:, hs0:hs0 + 8], 64, 2), ALU.mult, [bx, b_rows], [by2])
                yield
                psy, pby = psum()
                for r in range(8):
                    MM(psy[:, r * 64:(r + 1) * 64], MT[:, r, :], xdt[:, r * 64:(r + 1) * 64], True, True, [bMT, bxdt], [pby])
                pso, pbo = psum()
                MM(pso[:, :], CT_t[gb][:, tok], Sb, True, True, [b_CT[gb], bSb], [pbo])
                k = rotate("y", 4)
                y1, by1 = y_t[k], b_y[k]
                TT("dve", v3(y1, 64), v3(pso[:, :], 64), bc3(od_i, 64, 2), ALU.mult, [pbo, bod], [by1])
                TT("dve", y1, y1, psy[:, :], ALU.add, [by1, pby], [by1])
                TT("dve", y1, y1, y2, ALU.add, [by1, by2], [by1])
                TT("dve", y1, y1, sz_t[gb][:, i, :], ALU.mult, [by1, b_sz[gb]], [by1])
                ssq, bssq = sm()
                ACT(junk[:, 0:512], y1, AF.Square, [by1], [bssq], scale=float(512 ** -0.5), accum=ssq)
                rstd, brstd = rstd_from_ssq(ssq, bssq)
                ACT(y2, y1, AF.Copy, [by1, brstd], [by2], scale=rstd)
                k = rotate("si", 2)
                si, bsi = si_t[k], b_si[k]
                TT("pool", v3(si, 64), xs3, bc3(w_i, 64, 2), ALU.mult, [bx, bw], [bsi])
                yield
                pt, pbt = psum()
                for kk in range(4):
                    TR(pt[:, kk * 128:(kk + 1) * 128], y2[:, kk * 128:(kk + 1) * 128], identf[:], [by2, b_identf], [pbt])
                TT("dve", ymT[:, 4 * g:4 * g + 4, tok], v3(pt[:, :], 128),
                   bc3(cols[:, C_MBG + 4 * g:C_MBG + 4 * g + 4], 128, 2), ALU.mult, [pbt, b_cols], [b_ymT])
            if not full:
                k = rotate("si", 2)
                si, bsi = si_t[k], b_si[k]
                TT("pool", v3(si, 64), xs3, bc3(w_i, 64, 2), ALU.mult, [bx, bw], [bsi])
            psu, pbu = psum()
            MM(psu[:, :], B_t, si, True, True, [bx, bsi], [pbu])
            TT("pool", v3(Sg, 64), v3(Sg, 64), bc3(cd_i, 64, 2), ALU.mult, [b_S[g], bcd], [b_S[g]])
            TT("dve", Sg, Sg, psu[:, :], ALU.add, [b_S[g], pbu], [b_S[g]])
            yield

        def group_B(g, full):
            for i in range(NTB):
                yield from ssd_tile(g, i, full, g % 2)

        def group_phase(g, full):
            gb = g % 2
            xsB, bx = xsB_t[gb], b_xsB[gb]
            chunks = [("x", g * 4 + c, c) for c in range(4)] + [("B", 32 + g, 4)]
            if full:
                chunks.append(("C", 40 + g, 5))
            pending = None
            for kind, j, c in chunks:
                ps, pb = proj_S([S_XBC + j], xnT, b_xnT)

                def src_fn(dst, bm, ps=ps, pb=pb):
                    CP("act", dst, ps[:, 0:T], [pb], [bm])
                acc, ba = conv(src_fn, 4, cols[:, C_MBW + 4 * j:C_MBW + 4 * j + 4], cols[:, C_MBB + j:C_MBB + j + 1],
                               xbch[:, j, :], b_xbch[j])
                if kind == "C":
                    ACT(CT_t[gb], acc[:], AF.Silu, [ba], [b_CT[gb]])
                else:
                    if kind == "B":
                        fT, bfT = BT_t[gb], b_BT[gb]
                    else:
                        k = rotate("fT", 2)
                        fT, bfT = fT_t[k], b_fT[k]
                    ACT(fT, acc[:], AF.Silu, [ba], [bfT])
                if pending is not None:
                    pending()
                    pending = None
                if kind != "C":
                    def pending(fT=fT, bfT=bfT, c=c):
                        pt, pbt = psum()
                        ptb = v3(pt[:, :].bitcast(BF16)[:, 0:T], 128)
                        for i in range(NTB):
                            TR(ptb[:, i, :], fT[:, i * 128:(i + 1) * 128], identb[:], [bfT, b_identb], [pbt])
                        CP("dve", xsB[:, :, c * 128:(c + 1) * 128], ptb, [pbt], [bx])
                yield
            if pending is not None:
                pending()
                pending = None
            if full:
                zps = [psum() for _ in range(NTB)]
                for kg in range(4):
                    slot, bs = wload("M", M_Z + g * 4 + kg)
                    for i in range(NTB):
                        for kcl in range(4):
                            kc = kg * 4 + kcl
                            MM(zps[i][0][:, :], xnT[:, kc, i * 128:(i + 1) * 128], slot[:, kcl * 512:(kcl + 1) * 512],
                               kc == 0, kc == KC - 1, [b_xnT, bs], [zps[i][1]])
                for i in range(NTB):
                    ACT(sz_t[gb][:, i, :], zps[i][0][:, :], AF.Silu, [zps[i][1]], [b_sz[gb]])
            yield

        def sc_phase():
            for i in range(16):
                psc, pbc = proj_S([S_SC + 3 * i], xnT, b_xnT)
                psh, pbh = proj_S([S_SC + 3 * i + 1], xnT, b_xnT)
                psb, pbb = proj_S([S_SC + 3 * i + 2], xnT, b_xnT)
                ft, bft = ftbuf()
                CP("act", ft[:], psc[:, 0:T], [pbc], [bft])

                def src_fn(dst, bm, ft=ft, bft=bft, psh=psh, pbh=pbh):
                    TT("dve", dst, ft[:], psh[:, 0:T], ALU.mult, [bft, pbh], [bm])
                acc, ba = conv(src_fn, 3, cols[:, C_SCW + 3 * i:C_SCW + 3 * i + 3], None, sch[:, i, :], b_sch[i])
                TT("dve", yaT[:, i, :], acc[:], psb[:, 0:T], ALU.mult, [ba, pbb], [b_yaT])

        def mix_phase():
            for i in range(16):
                psa, pba = proj_S([S_WA + i], yaT, b_yaT)
                psm, pbm = proj_S([S_WM + 2 * i, S_WM + 2 * i + 1], ymT, b_ymT)
                pga, pbga = proj_S([S_GATE + i], xnT, b_xnT)
                pgm, pbgm = proj_S([S_GATE + 16 + i], xnT, b_xnT)
                ga, bga = ftbuf()
                gm, bgm = ftbuf()
                ACT(ga[:], pga[:, 0:T], AF.Sigmoid, [pbga, b_cols], [bga], bias=cols[:, C_BG + i:C_BG + i + 1])
                ACT(gm[:], pgm[:, 0:T], AF.Sigmoid, [pbgm, b_cols], [bgm], bias=cols[:, C_BG + 16 + i:C_BG + 16 + i + 1])
                TT("dve", ga[:], ga[:], psa[:, 0:T], ALU.mult, [bga, pba], [bga])
                TT("dve", gm[:], gm[:], psm[:, 0:T], ALU.mult, [bgm, pbm], [bgm])
                TT("pool", mixT[:, i, :], ga[:], gm[:], ALU.add, [bga, bgm], [b_mixT])

        def tm_proj(unit0, nkg, nkc, lhsT_T, blhs, evac):
            for q in range(4):
                banks = [psum() for _ in range(NTB)]
                for kg in range(nkg):
                    slot, bs = wload("M", unit0 + q * nkg + kg)
                    for i in range(NTB):
                        for kcl in range(4):
                            kc = kg * 4 + kcl
                            if kc >= nkc:
                                continue
                            MM(banks[i][0][:, :], lhsT_T[:, kc, i * 128:(i + 1) * 128], slot[:, kcl * 512:(kcl + 1) * 512],
                               kc == 0, kc == nkc - 1, [blhs, bs], [banks[i][1]])
                for i in range(NTB):
                    evac(q, i, banks[i][0], banks[i][1])

        def ffn_up_phase():
            for i in range(NFC):
                psu, pbu = proj_S([S_UP + 2 * i], xnT, b_xnT)
                psv, pbv = proj_S([S_UP + 2 * i + 1], xnT, b_xnT)

                def src_fn(dst, bm, psu=psu, pbu=pbu):
                    CP("act", dst, psu[:, 0:T], [pbu], [bm])
                acc, ba = conv(src_fn, 3, cols[:, C_FFW + 3 * i:C_FFW + 3 * i + 3], cols[:, C_FFB + i:C_FFB + i + 1],
                               uh[:, i, :], b_uh[i])
                ft, bft = ftbuf()
                ACT(ft[:], acc[:], AF.Silu, [ba], [bft])
                TT("dve", actT[:, i, :], ft[:], psv[:, 0:T], ALU.mult, [bft, pbv], [b_actT])

        nblk_p = NPFX // NTB
        nblk_f = NF // NTB
        for blk in range(nblk_p + nblk_f):
            full = blk >= nblk_p
            t0 = blk * NTB
            if full:
                pc_tick(len(pc_pending))
            switch_alias(R2B, R2A)
            switch_alias(R1B, R1A)
            for i in range(NTB):
                tile = t0 + i
                k = tile % 2
                P.op("sp", lambda e, k=k, tile=tile: e.dma_start(out=xld[k][:], in_=xin[tile * 128:(tile + 1) * 128, :]),
                     (), [b_xld[k]], dsem=d_x[k])
                norm_transpose(xld[k][:], b_xld[k], xld[k][:], b_xld[k], C_G1, xnT, b_xnT, i)
            dt_phase(t0)
            for _ in group_phase(0, full):
                pass
            for g in range(NG):
                gB = group_B(g, full)
                gA = group_phase(g + 1, full) if g + 1 < NG else iter(())
                doneA = doneB = False
                while not (doneA and doneB):
                    if not doneB:
                        try:
                            next(gB)
                        except StopIteration:
                            doneB = True
                    if not doneA:
                        try:
                            next(gA)
                        except StopIteration:
                            doneA = True
            if not full:
                pc_tick(-(-320 // max(1, nblk_p - 1)))
                continue
            sc_phase()
            switch_alias(R2A, R2B)
            mix_phase()
            for i in range(NTB):
                tile = t0 + i
                P.op("pool", lambda e, i=i, tile=tile: e.dma_start(out=h1[:, i, :], in_=xin[tile * 128:(tile + 1) * 128, :]),
                     (), [b_h1[i]], dsem=d_h[i])

            def evac_h(q, i, ps, pb):
                TT("dve", h1[:, i, q * 512:(q + 1) * 512], ps[:, :], h1[:, i, q * 512:(q + 1) * 512], ALU.add,
                   [pb, b_h1[i]], [b_h1[i]])
            tm_proj(M_WO, 4, KC, mixT, b_mixT, evac_h)
            for i in range(NTB):
                k = i % 2
                norm_transpose(h1[:, i, :], b_h1[i], xld[k][:], b_xld[k], C_G2, xnT, b_xnT, i)
            switch_alias(R1A, R1B)
            ffn_up_phase()
            tm_proj(M_DN, 11, NFC, actT, b_actT, evac_h)
            for i in range(NTB):
                ftile = (blk - nblk_p) * NTB + i
                if ftile == 0 or ftile > NT:
                    continue
                k = i % 2
                ssq, bssq = sm()
                ACT(junk[:], h1[:, i, :], AF.Square, [b_h1[i]], [bssq], scale=float(D ** -0.5), accum=ssq)
                rstd, brstd = rstd_from_ssq(ssq, bssq)
                STT("dve", xld[k][:], h1[:, i, :], rstd, gfbc[:], ALU.mult, ALU.mult, [b_h1[i], brstd, b_gf], [b_xld[k]])
                orow = (ftile - 1) * 128
                P.op("pool", lambda e, k=k, orow=orow: e.dma_start(out=outd[orow:orow + 128, :], in_=xld[k][:]),
                     [b_xld[k]], (), dsem=d_o[k])
        P.run(final_dsems=d_o)
        nc._sem_counts = dict(P.cnt)
    return nc


def _s_unit(W, col0, k0=0):
    blk = W[k0:k0 + 2048, col0:col0 + 128]
    return blk.reshape(16, 128, 128).transpose(1, 0, 2).reshape(128, 2048)


def _m_unit(W, col0, k0):
    K = W.shape[0]
    blk = np.zeros((512, 512), np.float32)
    n = max(0, min(512, K - k0))
    blk[:n] = W[k0:k0 + n, col0:col0 + 512]
    return blk.reshape(4, 128, 512).transpose(1, 0, 2).reshape(128, 2048)


def _colp(v):
    return np.ascontiguousarray(v.reshape(-1, 128).T)


def _convp(w):
    K, C = w.shape
    return np.ascontiguousarray(w.T.reshape(C // 128, 128, K).transpose(1, 0, 2).reshape(128, -1))


def prep_weights(norm1_g, w_in, b_gate, sc_conv_w, mb_conv_w, mb_conv_b, dt_bias, a_log, d_skip, mb_norm_g,
                 w_a, w_m, w_o, norm2_g, w_up, ffn_conv_w, ffn_conv_b, w_down, normf_g):
    f = lambda a: np.asarray(a, np.float32)
    w_in, w_a, w_m, w_o, w_up, w_down = f(w_in)[0], f(w_a)[0], f(w_m)[0], f(w_o)[0], f(w_up)[0], f(w_down)[0]
    WS = np.empty((NSU, 128, 2048), np.float32)
    WM = np.empty((NMU, 128, 2048), np.float32)
    o_scb, o_scc, o_sch, o_z, o_xbc, o_dt, o_gate = 0, 2048, 4096, 6144, 10240, 16384, 16448
    for i in range(16):
        WS[S_SC + 3 * i] = _s_unit(w_in, o_scc + i * 128)
        WS[S_SC + 3 * i + 1] = _s_unit(w_in, o_sch + i * 128)
        WS[S_SC + 3 * i + 2] = _s_unit(w_in, o_scb + i * 128)
    for j in range(48):
        WS[S_XBC + j] = _s_unit(w_in, o_xbc + j * 128)
    for j in range(32):
        WS[S_GATE + j] = _s_unit(w_in, o_gate + j * 128)
    for i in range(16):
        WS[S_WA + i] = _s_unit(w_a, i * 128)
        WS[S_WM + 2 * i] = _s_unit(w_m, i * 128, 0)
        WS[S_WM + 2 * i + 1] = _s_unit(w_m, i * 128, 2048)
    for i in range(NFC):
        WS[S_UP + 2 * i] = _s_unit(w_up, i * 128)
        WS[S_UP + 2 * i + 1] = _s_unit(w_up, DFF + i * 128)
    for g in range(8):
        for kg in range(4):
            WM[M_Z + g * 4 + kg] = _m_unit(w_in, o_z + g * 512, kg * 512)
    for q in range(4):
        for kg in range(4):
            WM[M_WO + q * 4 + kg] = _m_unit(w_o, q * 512, kg * 512)
        for kg in range(11):
            WM[M_DN + q * 11 + kg] = _m_unit(w_down, q * 512, kg * 512)
    WDT = np.ascontiguousarray(w_in[:, o_dt:o_dt + 64].reshape(16, 128, 64).transpose(1, 0, 2).reshape(128, 1024))
    cols = np.zeros((128, NCOLS), np.float32)
    cols[:, C_G1:C_G1 + 16] = _colp(f(norm1_g)[0])
    cols[:, C_G2:C_G2 + 16] = _colp(f(norm2_g)[0])
    cols[:, C_SCW:C_SCW + 48] = _convp(f(sc_conv_w)[0])
    cols[:, C_MBW:C_MBW + 192] = _convp(f(mb_conv_w)[0])
    cols[:, C_MBB:C_MBB + 48] = _colp(f(mb_conv_b)[0])
    cols[:, C_FFW:C_FFW + 129] = _convp(f(ffn_conv_w)[0])
    cols[:, C_FFB:C_FFB + 43] = _colp(f(ffn_conv_b)[0])
    cols[:, C_BG:C_BG + 32] = _colp(f(b_gate)[0])
    cols[:, C_MBG:C_MBG + 32] = _colp(f(mb_norm_g)[0])
    rows = np.concatenate([f(dt_bias)[0], f(a_log)[0], f(d_skip)[0]]).astype(np.float32)
    return {"WS": WS, "WM": WM, "WDT": WDT, "cols": cols, "rows": rows, "gf": np.ascontiguousarray(f(normf_g))}


def prep_streams(x, meta_tokens, NT, CPB, NTB):
    x = np.asarray(x, np.float32)
    meta = np.asarray(meta_tokens, np.float32)
    B = x.shape[0]
    NPFX, NF = _rup((CPB - 1) * NT, NTB), _rup(NT + 1, NTB)
    NTILES = NPFX + NF
    NEND = NPFX + NT + 1
    streams, masks = [], []
    for b in range(B):
        seq = np.zeros(((CPB * NT + 1) * 128, D), np.float32)
        seq[128 - N_META:128] = meta
        seq[128:] = x[b]
        valid = np.ones(((CPB * NT + 1) * 128,), np.float32)
        valid[:128 - N_META] = 0.0
        for j in range(CPB):
            n_real = (j + 1) * NT + 1
            s = np.zeros((NTILES * 128, D), np.float32)
            m = np.zeros((NTILES * 128,), np.float32)
            s[(NEND - n_real) * 128:NEND * 128] = seq[:n_real * 128]
            m[(NEND - n_real) * 128:NEND * 128] = valid[:n_real * 128]
            streams.append(s)
            masks.append(np.ascontiguousarray(m.reshape(NTILES, 128).T))
    return streams, masks


_CACHE = {}


def run(inputs, NT, NTB, CPB):
    key = (NT, NTB, CPB)
    if key not in _CACHE:
        _CACHE[key] = build(NT, NTB, CPB)
    nc = _CACHE[key]
    wd = prep_weights(**{k: v for k, v in inputs.items() if k not in ("x", "meta_tokens")})
    streams, masks = prep_streams(inputs["x"], inputs["meta_tokens"], NT, CPB, NTB)
    B = 2
    ncores = B * CPB
    in_maps = []
    for c in range(ncores):
        m = dict(wd)
        m["xin"] = streams[c]
        m["mask"] = masks[c]
        in_maps.append(m)
    res = run_bass_kernel_spmd(nc, in_maps, core_ids=list(range(ncores)))
    out = np.empty((B, CPB * NT * 128, D), np.float32)
    for c in range(ncores):
        b, j = divmod(c, CPB)
        out[b, j * NT * 128:(j + 1) * NT * 128] = res.results[c]["out"]
    return out


CPB_FULL = 4


def kernel(**inputs):
    return run(inputs, SEQ // 128 // CPB_FULL, 3, CPB_FULL)
```
